# Optimizing a Trainium2 kernel written in Bass

```python
import jax, jax.numpy as jnp
from jax import lax
import numpy as np

D_MODEL = 2048
BATCH = 16
SEQ = 256
DEPTH = 4
DEC_BATCH = 8
DEC_SEQ = 4096
PAST_LEN = 512

GRID_W = 64
N_MLA_LAYERS = (DEPTH + 1) // 2
N_SG_LAYERS = DEPTH // 2
EPS = 1e-6
CONV_WIDTH = 2048
MLA_HEADS = 16
QK_NOPE = 128
QK_ROPE = 64
V_HEAD = 128
Q_LORA = 512
KV_LORA = 512
ROPE_BASE = 10000.0
Q_BLOCK = 128
MLA_WIDTH = MLA_HEADS * V_HEAD
CHUNK = 128
SG_WIDTH = 4096
SG_GROUPS = 16
EVEN_IN = 4 * CONV_WIDTH + Q_LORA + KV_LORA + QK_ROPE + MLA_WIDTH
EVEN_SPLITS = (CONV_WIDTH, 2 * CONV_WIDTH, 3 * CONV_WIDTH, 4 * CONV_WIDTH,
               4 * CONV_WIDTH + Q_LORA, 4 * CONV_WIDTH + Q_LORA + KV_LORA,
               4 * CONV_WIDTH + Q_LORA + KV_LORA + QK_ROPE)
EVEN_MIX = CONV_WIDTH + MLA_WIDTH
ODD_IN = 3 * SG_WIDTH

kernel_name = 'hybrid_diffusion_conv_mla_sgmlp_step'

F32 = jnp.float32


def rmsnorm(x, g):
    x32 = x.astype(F32)
    y = x32 * lax.rsqrt(jnp.mean(x32 * x32, axis=-1, keepdims=True) + EPS) * g.astype(F32)
    return y.astype(x.dtype)


def layernorm(x, g, b):
    x32 = x.astype(F32)
    mu = jnp.mean(x32, axis=-1, keepdims=True)
    xc = x32 - mu
    var = jnp.mean(xc * xc, axis=-1, keepdims=True)
    return (xc * lax.rsqrt(var + EPS) * g.astype(F32) + b.astype(F32)).astype(x.dtype)


def ada_mod(cond, w, b):
    m = jax.nn.silu(cond) @ w + b
    return jnp.split(m[:, None, :], 3, axis=-1)


def dwconv3(x, w):
    xp = jnp.pad(x, ((0, 0), (1, 1), (0, 0)))
    return xp[:, :-2] * w[0] + xp[:, 1:-1] * w[1] + xp[:, 2:] * w[2]


def axial_rope_tables(n):
    rows = n // GRID_W
    row = jnp.repeat(jnp.arange(rows, dtype=F32), GRID_W)
    col = jnp.tile(jnp.arange(GRID_W, dtype=F32), rows)
    n_freq = QK_ROPE // 4
    inv = ROPE_BASE ** (-jnp.arange(n_freq, dtype=F32) / n_freq)
    ar = row[:, None] * inv
    ac = col[:, None] * inv
    ang = jnp.concatenate([ar, ar, ac, ac], axis=-1)
    return jnp.cos(ang), jnp.sin(ang)


def apply_rope(x, cos, sin):
    xr = x.reshape(x.shape[:-1] + (2, 2, QK_ROPE // 4))
    rot = jnp.stack([-xr[..., 1, :], xr[..., 0, :]], axis=-2).reshape(x.shape)
    return (x.astype(F32) * cos + rot.astype(F32) * sin).astype(x.dtype)


def mla_attend(q_nope, q_pe, k_nope, k_pe, v):
    bsz, n, h, _ = q_nope.shape
    nb = n // Q_BLOCK
    qn = q_nope.reshape(bsz, nb, Q_BLOCK, h, QK_NOPE).transpose(1, 0, 2, 3, 4)
    qp = q_pe.reshape(bsz, nb, Q_BLOCK, h, QK_ROPE).transpose(1, 0, 2, 3, 4)
    scale = (QK_NOPE + QK_ROPE) ** -0.5

    def block(args):
        qn_b, qp_b = args
        s = (jnp.einsum('bqhd,bkhd->bhqk', qn_b, k_nope, preferred_element_type=F32)
             + jnp.einsum('bqhr,bkr->bhqk', qp_b, k_pe, preferred_element_type=F32))
        p = jax.nn.softmax(s * scale, axis=-1)
        return jnp.einsum('bhqk,bkhd->bqhd', p.astype(v.dtype), v)

    o = lax.map(block, (qn, qp))
    return o.transpose(1, 0, 2, 3, 4).reshape(bsz, n, h * V_HEAD)


def even_mixer(h, w_in, conv_w, q_norm_g, w_qb, kv_norm_g, w_kvb, w_out, rope, ctx):
    bsz, n, _ = h.shape
    cb, cc, cx, cg, q_a, ckv_raw, kpe, mg = jnp.split(h @ w_in, EVEN_SPLITS, axis=-1)
    y_conv = cb * dwconv3(cc * cx, conv_w) * jax.nn.silu(cg)
    q = (rmsnorm(q_a, q_norm_g) @ w_qb).reshape(bsz, n, MLA_HEADS, QK_NOPE + QK_ROPE)
    q_nope, q_pe = q[..., :QK_NOPE], q[..., QK_NOPE:]
    ckv = rmsnorm(ckv_raw, kv_norm_g)
    if rope is None:
        kpe_keys = kpe
    else:
        cos, sin = rope
        q_pe = apply_rope(q_pe, cos[:, None, :], sin[:, None, :])
        kpe_keys = apply_rope(kpe, cos, sin)
    if ctx is None:
        ckv_all, kpe_all = ckv, kpe_keys
    else:
        ckv_all = jnp.concatenate([ctx[0], ckv], axis=1)
        kpe_all = jnp.concatenate([ctx[1], kpe_keys], axis=1)
    kv = (ckv_all @ w_kvb).reshape(bsz, ckv_all.shape[1], MLA_HEADS, QK_NOPE + V_HEAD)
    k_nope, v = kv[..., :QK_NOPE], kv[..., QK_NOPE:]
    y_att = mla_attend(q_nope, q_pe, k_nope, kpe_all, v) * jax.nn.silu(mg)
    out = jnp.concatenate([y_conv, y_att], axis=-1) @ w_out
    return out, ckv, kpe


def odd_mixer(h, w_in, ln_g, ln_b, w_s, b_s, w_out):
    bsz, n, _ = h.shape
    u, v, g = jnp.split(h @ w_in, 3, axis=-1)
    v = layernorm(v, ln_g, ln_b)
    vc = v.reshape(bsz, n // CHUNK, CHUNK, SG_GROUPS, SG_WIDTH // SG_GROUPS)
    vs = jnp.einsum('gpq,bcqgd->bcpgd', w_s, vc) + b_s.T[:, :, None]
    y = u * vs.reshape(bsz, n, SG_WIDTH) * jax.nn.silu(g)
    return y @ w_out


def trunk(x, cond, rope, cache_ckv, cache_kpe, norm_g, w_ada, b_ada,
          e_w_in, e_conv_w, e_q_norm_g, e_w_qb, e_kv_norm_g, e_w_kvb, e_w_out,
          o_w_in, o_ln_g, o_ln_b, o_w_s, o_b_s, o_w_out, final_g):
    ckvs, kpes = [], []
    for l in range(DEPTH):
        shift, scale, gate = ada_mod(cond, w_ada[l], b_ada[l])
        h = rmsnorm(x, norm_g[l]) * (1 + scale) + shift
        i = l // 2
        if l % 2 == 0:
            ctx = None if cache_ckv is None else (cache_ckv[:, i], cache_kpe[:, i])
            out, ckv, kpe = even_mixer(h, e_w_in[i], e_conv_w[i], e_q_norm_g[i], e_w_qb[i],
                                       e_kv_norm_g[i], e_w_kvb[i], e_w_out[i], rope, ctx)
            ckvs.append(ckv)
            kpes.append(kpe)
        else:
            out = odd_mixer(h, o_w_in[i], o_ln_g[i], o_ln_b[i], o_w_s[i], o_b_s[i], o_w_out[i])
        x = x + gate * out
    return rmsnorm(x, final_g), ckvs, kpes


def setup_inputs(seed: int = 0) -> dict:
    key = jax.random.key(seed)
    ks = jax.random.split(key, 24)
    nrm = jax.random.normal
    D = D_MODEL
    return {
        'x_prompt': nrm(ks[0], (BATCH, SEQ, D), F32),
        'x_sample': nrm(ks[1], (DEC_BATCH, DEC_SEQ, D), F32),
        'cache_ckv': nrm(ks[2], (DEC_BATCH, N_MLA_LAYERS, PAST_LEN, KV_LORA), F32),
        'cache_kpe': nrm(ks[3], (DEC_BATCH, N_MLA_LAYERS, PAST_LEN, QK_ROPE), F32),
        'c': nrm(ks[4], (DEC_BATCH, D), F32),
        'c_ctx': nrm(ks[5], (D,), F32),
        'norm_g': 1.0 + 0.02 * nrm(ks[6], (DEPTH, D), F32),
        'w_ada': nrm(ks[7], (DEPTH, D, 3 * D), F32) * D ** -0.5,
        'b_ada': 0.02 * nrm(ks[8], (DEPTH, 3 * D), F32),
        'e_w_in': nrm(ks[9], (N_MLA_LAYERS, D, EVEN_IN), F32) * D ** -0.5,
        'e_conv_w': nrm(ks[10], (N_MLA_LAYERS, 3, CONV_WIDTH), F32) * 3 ** -0.5,
        'e_q_norm_g': 1.0 + 0.02 * nrm(ks[11], (N_MLA_LAYERS, Q_LORA), F32),
        'e_w_qb': nrm(ks[12], (N_MLA_LAYERS, Q_LORA, MLA_HEADS * (QK_NOPE + QK_ROPE)), F32) * Q_LORA ** -0.5,
        'e_kv_norm_g': 1.0 + 0.02 * nrm(ks[13], (N_MLA_LAYERS, KV_LORA), F32),
        'e_w_kvb': nrm(ks[14], (N_MLA_LAYERS, KV_LORA, MLA_HEADS * (QK_NOPE + V_HEAD)), F32) * KV_LORA ** -0.5,
        'e_w_out': nrm(ks[15], (N_MLA_LAYERS, EVEN_MIX, D), F32) * EVEN_MIX ** -0.5,
        'o_w_in': nrm(ks[16], (N_SG_LAYERS, D, ODD_IN), F32) * D ** -0.5,
        'o_ln_g': 1.0 + 0.02 * nrm(ks[17], (N_SG_LAYERS, SG_WIDTH), F32),
        'o_ln_b': 0.02 * nrm(ks[18], (N_SG_LAYERS, SG_WIDTH), F32),
        'o_w_s': nrm(ks[19], (N_SG_LAYERS, SG_GROUPS, CHUNK, CHUNK), F32) * CHUNK ** -0.5,
        'o_b_s': 1.0 + 0.1 * nrm(ks[20], (N_SG_LAYERS, SG_GROUPS, CHUNK), F32),
        'o_w_out': nrm(ks[21], (N_SG_LAYERS, SG_WIDTH, D), F32) * SG_WIDTH ** -0.5,
        'final_g': 1.0 + 0.02 * nrm(ks[22], (D,), F32),
    }


def reference(x_prompt, x_sample, cache_ckv, cache_kpe, c, c_ctx, norm_g, w_ada, b_ada,
              e_w_in, e_conv_w, e_q_norm_g, e_w_qb, e_kv_norm_g, e_w_kvb, e_w_out,
              o_w_in, o_ln_g, o_ln_b, o_w_s, o_b_s, o_w_out, final_g):
    y_prompt, ckvs, kpes = trunk(x_prompt, c_ctx[None, :], None, None, None, norm_g, w_ada, b_ada,
                                 e_w_in, e_conv_w, e_q_norm_g, e_w_qb, e_kv_norm_g, e_w_kvb, e_w_out,
                                 o_w_in, o_ln_g, o_ln_b, o_w_s, o_b_s, o_w_out, final_g)
    new_ckv = jnp.stack(ckvs, axis=1)
    new_kpe = jnp.stack(kpes, axis=1)
    rope = axial_rope_tables(x_sample.shape[1])
    y_sample, _, _ = trunk(x_sample, c, rope, cache_ckv, cache_kpe, norm_g, w_ada, b_ada,
                           e_w_in, e_conv_w, e_q_norm_g, e_w_qb, e_kv_norm_g, e_w_kvb, e_w_out,
                           o_w_in, o_ln_g, o_ln_b, o_w_s, o_b_s, o_w_out, final_g)
    return (y_prompt, y_sample, new_ckv, new_kpe)
```

```python
import numpy as np
from contextlib import ExitStack
import concourse.bass as bass
import concourse.mybir as mybir
from concourse.bass_utils import run_bass_kernel_spmd

F32 = mybir.dt.float32
BF16 = mybir.dt.bfloat16
AF = mybir.ActivationFunctionType
ALU = mybir.AluOpType

D = 2048
KC = 16
T = 4608
NS = 4096
NKEY = 5120
EPS = 1e-6
SCALE = 192 ** -0.5
E_NP = 23
O_NP = 24
DEPTH = 4


class Buf:
    __slots__ = ("name", "w", "r")

    def __init__(self, name=""):
        self.name = name
        self.w = {}
        self.r = {}


class FW:
    NDMA = 48

    def __init__(self, nc, es):
        self.nc = nc
        self.eng = {"pe": nc.tensor, "act": nc.scalar, "dve": nc.vector, "pool": nc.gpsimd, "sp": nc.sync}
        self.sem = {}
        self.cnt = {}
        for e in ("pe", "act", "dve", "pool"):
            self.sem[e] = es.enter_context(nc.semaphore("s_" + e))
            self.cnt[e] = 0
        self.dpool = {"sp": [], "pool": []}
        for q, n in (("sp", 32), ("pool", 24)):
            for i in range(n):
                k = ("d" + q, i)
                self.sem[k] = es.enter_context(nc.semaphore("s_d%s%d" % (q, i)))
                self.cnt[k] = 0
                self.dpool[q].append(k)
        self.dnext = {"sp": 0, "pool": 0}
        self.seen = {e: {} for e in self.eng}
        self.out_toks = []

    def _waits(self, eng, reads, writes, extra=()):
        need = {}

        def add(k, v):
            if need.get(k, 0) < v:
                need[k] = v
        for b in reads:
            for k, v in b.w.items():
                add(k, v)
        for b in writes:
            for k, v in b.w.items():
                add(k, v)
            for k, v in b.r.items():
                add(k, v)
        if eng == "pe":
            need.pop("pe", None)
        for k, v in extra:
            add(k, v)
        seen = self.seen[eng]
        e = self.eng[eng]
        for k, v in need.items():
            if seen.get(k, 0) < v:
                seen[k] = v
                e.wait_ge(self.sem[k], v)

    def _commit(self, tok, reads, writes):
        k, v = tok
        for b in writes:
            b.w[k] = v
        for b in reads:
            b.r[k] = v

    def op(self, eng, fn, reads=(), writes=()):
        self._waits(eng, reads, writes)
        ins = fn(self.eng[eng])
        self.cnt[eng] += 1
        ins.then_inc(self.sem[eng], 1)
        tok = (eng, self.cnt[eng])
        self._commit(tok, reads, writes)
        return tok

    def dma(self, q, out, in_, reads=(), writes=(), is_output=False):
        pl = self.dpool[q]
        k = pl[self.dnext[q] % len(pl)]
        self.dnext[q] += 1
        extra = [(k, self.cnt[k])] if self.cnt[k] else []
        self._waits(q, reads, writes, extra)
        ins = self.eng[q].dma_start(out=out, in_=in_)
        self.cnt[k] += 16
        ins.then_inc(self.sem[k], 16)
        tok = (k, self.cnt[k])
        self._commit(tok, reads, writes)
        if is_output:
            self.out_toks.append(tok)
        return tok

    def barrier(self):
        toks = [(k, v) for k, v in self.cnt.items() if v > 0]
        for eng in self.eng:
            seen = self.seen[eng]
            e = self.eng[eng]
            for k, v in toks:
                if k == eng:
                    continue
                if seen.get(k, 0) < v:
                    seen[k] = v
                    e.wait_ge(self.sem[k], v)


def sample_tiles():
    tiles = []
    step = 456
    lo = 0
    while lo < NS:
        hi = min(lo + step, NS)
        c0 = max(lo - 1, 0)
        c1 = min(hi + 1, NS)
        tiles.append(dict(c0=c0, n=c1 - c0, outs=[(lo - c0, hi - c0, 0, c1 - c0, lo == 0, hi == NS)],
                          cond=0, rope=True))
        lo = hi
    return tiles


def build_program(depth=DEPTH, stop=None):
    nc = bass.Bass("TRN2", target_bir_lowering=False)
    top = ExitStack()
    fw = FW(nc, top)

    def din(name, shape, dt=F32):
        return nc.dram_tensor(name, list(shape), dt, kind="ExternalInput").ap()

    def dout(name, shape, dt=F32):
        return nc.dram_tensor(name, list(shape), dt, kind="ExternalOutput").ap()

    def dint(name, shape, dt):
        return nc.dram_tensor(name, list(shape), dt).ap()

    x_tok = din("x_tok", [T, D])
    cckv = din("cckv", [2, 512, 512])
    ckpe = din("ckpe", [2, 512, 64])
    condT = din("condT", [128, KC * 2])
    ngT = din("ngT", [128, 5 * KC])
    w_ada = din("w_ada", [DEPTH, D, 3 * D])
    b_adaT = din("b_adaT", [128, DEPTH * 48])
    e_win = din("e_win", [2, E_NP * 128, KC * 512])
    e_whd = din("e_whd", [2, 16 * 128, 4 * 512])
    e_wout = din("e_wout", [2, 8 * 128, 32 * 256])
    e_small = din("e_small", [128, 2 * (48 + 4 + 4)])
    o_win = din("o_win", [2, O_NP * 128, KC * 512])
    o_wout = din("o_wout", [2, 16 * 128, 32 * 128])
    o_wsT = din("o_wsT", [2, 128, 16 * 128])
    o_small = din("o_small", [128, 2 * 64])
    o_bs = din("o_bs", [2, 1, 16 * 128])
    ropeT = din("ropeT", [2, 64, NS])
    ident_d = din("ident", [128, 128])

    y_tok = dout("y_tok", [T, D])
    new_ckv = dout("new_ckv", [2, 2, 256, 512])
    new_kpe = dout("new_kpe", [2, 2, 256, 64])

    xT = dint("xT", [D, T], F32)
    qnT = dint("qnT", [512, T], BF16)
    smgT = dint("smgT", [D, T], BF16)
    ymixT = dint("ymixT", [2 * D, T], BF16)
    e_win_b = dint("e_win_b", [2, E_NP * 128, KC * 512], BF16)
    e_whd_b = dint("e_whd_b", [2, 16 * 128, 4 * 512], BF16)
    e_wout_b = dint("e_wout_b", [2, 8 * 128, 32 * 256], BF16)
    o_win_b = dint("o_win_b", [2, O_NP * 128, KC * 512], BF16)
    o_wout_b = dint("o_wout_b", [2, 16 * 128, 32 * 128], BF16)
    B_xT = Buf("xT")
    B_qnT = Buf("qnT")
    B_smgT = Buf("smgT")
    B_ymix = Buf("ymix")
    B_wb = {n: [Buf(n + "0"), Buf(n + "1")] for n in ("e_win", "e_whd", "e_wout", "o_win", "o_wout")}

    uid = [0]

    def S(es, name, shape, dt=F32):
        uid[0] += 1
        return es.enter_context(nc.sbuf_tensor("%s_%d" % (name, uid[0]), list(shape), dt))

    ident = S(top, "ident", [128, 128])
    ones_f = S(top, "ones_f", [128, 128])
    ones_b = S(top, "ones_b", [128, 128], BF16)
    epsb = S(top, "epsb", [128, 1])
    Gt = S(top, "Gt", [128, DEPTH * KC * 2])
    SHt = S(top, "SHt", [128, DEPTH * KC * 2])
    GAt = S(top, "GAt", [128, DEPTH * KC * 2])
    ngs = S(top, "ngs", [128, 5 * KC])
    esm = S(top, "esm", [128, 2 * 56])
    osm = S(top, "osm", [128, 2 * 64])
    B_const = Buf("const")
    B_mod = Buf("mod")
    pbig = [top.enter_context(nc.psum_tensor("pb%d" % i, [128, 1024], F32)) for i in range(4)]
    psum = [pbig[i // 2][:, (i % 2) * 512:(i % 2) * 512 + 512] for i in range(8)]
    B_ps = [Buf("ps%d" % i) for i in range(8)]

    def mod_ap(t, l, f, cond):
        o = (l * KC + f) * 2 + cond
        return t[:, o:o + 1]

    fw.op("pool", lambda e: e.memset(ones_f[:], 1.0), writes=[B_const])
    fw.op("pool", lambda e: e.memset(ones_b[:], 1.0), writes=[B_const])
    fw.op("pool", lambda e: e.memset(epsb[:], EPS), writes=[B_const])
    fw.dma("sp", ident[:], ident_d, writes=[B_const])
    fw.dma("sp", ngs[:], ngT, writes=[B_const])
    fw.dma("sp", esm[:], e_small, writes=[B_const])
    fw.dma("sp", osm[:], o_small, writes=[B_const])

    pieces = []

    def add_precast(name, src, dst, i, lyr):
        rows = src.shape[1]
        bufs = []
        r = 0
        while r < rows:
            rr = min(256, rows - r)
            b = Buf()
            bufs.append(b)
            pieces.append((lyr, src[i, r:r + rr, :], dst[i, r:r + rr, :], b))
            r += rr
        B_wb[name][i] = bufs

    def bg_pump(n=1, upto=None):
        while pieces and (n > 0 or (upto is not None and pieces[0][0] <= upto)):
            lyr, src, dst, b = pieces.pop(0)
            fw.dma("pool", dst, src, writes=[b])
            n -= 1

    def wbuf(name, i, row0):
        return B_wb[name][i][row0 // 256]
    for l_ in range(depth):
        i_ = l_ // 2
        if l_ % 2 == 0:
            add_precast("e_win", e_win, e_win_b, i_, l_)
            add_precast("e_whd", e_whd, e_whd_b, i_, l_)
            add_precast("e_wout", e_wout, e_wout_b, i_, l_)
        else:
            add_precast("o_win", o_win, o_win_b, i_, l_)
            add_precast("o_wout", o_wout, o_wout_b, i_, l_)

    def pro_x():
        with ExitStack() as es:
            xin = [S(es, "xin%d" % i, [128, D]) for i in range(2)]
            B_xin = [Buf(), Buf()]
            xst = [S(es, "xst%d" % i, [128, KC * 128]) for i in range(2)]
            B_xst = [Buf(), Buf()]
            xT_v = xT.rearrange("(f p) t -> p f t", p=128)
            for st in range(T // 128):
                b = st % 2
                fw.dma("sp", xin[b][:], x_tok[st * 128:(st + 1) * 128, :], writes=[B_xin[b]])
                for g in range(4):
                    bk = (st * 4 + g) % 8

                    def tr(e, b=b, g=g, bk=bk):
                        for j in range(4):
                            f = g * 4 + j
                            ins = e.transpose(psum[bk][:, j * 128:(j + 1) * 128], xin[b][:, f * 128:(f + 1) * 128], ident[:])
                        return ins
                    fw.op("pe", tr, reads=[B_xin[b], B_const], writes=[B_ps[bk]])
                    eng = "act" if g % 2 == 0 else "dve"
                    if eng == "act":
                        fw.op("act", lambda e, b=b, g=g, bk=bk: e.activation(out=xst[b][:, g * 512:(g + 1) * 512], in_=psum[bk][:, 0:512], func=AF.Copy),
                              reads=[B_ps[bk]], writes=[B_xst[b]])
                    else:
                        fw.op("dve", lambda e, b=b, g=g, bk=bk: e.tensor_copy(out=xst[b][:, g * 512:(g + 1) * 512], in_=psum[bk][:, 0:512]),
                              reads=[B_ps[bk]], writes=[B_xst[b]])
                fw.dma("pool", xT_v[:, :, st * 128:(st + 1) * 128], xst[b][:].rearrange("p (f t) -> p f t", f=KC),
                       reads=[B_xst[b]], writes=[B_xT])

    sc = S(top, "sc", [128, KC * 2])
    macc = S(top, "macc", [128, 96])
    bad = S(top, "bad", [128, DEPTH * 48])
    B_sc = Buf()
    B_macc = Buf()
    ada_pending = []

    def ada_init():
        fw.dma("sp", sc[:], condT, writes=[B_sc])
        fw.dma("sp", bad[:], b_adaT, writes=[B_sc])
        fw.op("act", lambda e: e.activation(out=sc[:], in_=sc[:], func=AF.Silu), reads=[B_sc], writes=[B_sc])

    def ada_load(l, k, wa_b, B_wa_b):
        fw.dma("sp", wa_b[:], w_ada[l, k * 128:(k + 1) * 128, :], writes=[B_wa_b])

    def ada_kstep(l, k, wa_b, B_wa_b, bk, load=True):
        if load:
            ada_load(l, k, wa_b, B_wa_b)

        def mm(e):
            for j in range(48):
                ins = e.matmul(psum[bk][:, 2 * j:2 * j + 2], wa_b[:, j * 128:(j + 1) * 128], sc[:, 2 * k:2 * k + 2], start=True, stop=True)
            return ins
        fw.op("pe", mm, reads=[B_wa_b, B_sc], writes=[B_ps[bk]])
        if k == 0:
            fw.op("dve", lambda e: e.tensor_copy(out=macc[:], in_=psum[bk][:, 0:96]), reads=[B_ps[bk]], writes=[B_macc])
        else:
            fw.op("dve", lambda e: e.tensor_tensor(out=macc[:], in0=macc[:], in1=psum[bk][:, 0:96], op=ALU.add),
                  reads=[B_ps[bk], B_macc], writes=[B_macc])
        if k < KC - 1:
            return
        m3 = macc[:].rearrange("p (j c) -> p j c", c=2)
        for c in range(2):
            fw.op("dve", lambda e: e.tensor_tensor(out=m3[:, :, c], in0=m3[:, :, c], in1=bad[:, l * 48:(l + 1) * 48], op=ALU.add),
                  reads=[B_macc, B_sc], writes=[B_macc])
        lo = l * KC * 2
        G3 = Gt[:, lo:lo + 32].rearrange("p (f c) -> p f c", c=2)
        for c in range(2):
            fw.op("dve", lambda e: e.scalar_tensor_tensor(out=G3[:, :, c], in0=m3[:, 16:32, c], scalar=1.0,
                                                          in1=ngs[:, l * KC:(l + 1) * KC], op0=ALU.add, op1=ALU.mult),
                  reads=[B_macc, B_const], writes=[B_mod])
        fw.op("dve", lambda e: e.tensor_copy(out=SHt[:, lo:lo + 32], in_=macc[:, 0:32]), reads=[B_macc], writes=[B_mod])
        fw.op("dve", lambda e: e.tensor_copy(out=GAt[:, lo:lo + 32], in_=macc[:, 64:96]), reads=[B_macc], writes=[B_mod])

    def pro_ada():
        with ExitStack() as es:
            wa = [S(es, "wa%d" % i, [128, 3 * D]) for i in range(2)]
            B_wa = [Buf(), Buf()]
            ada_init()
            for k in range(KC):
                ada_kstep(0, k, wa[k % 2], B_wa[k % 2], k % 2)
        for l_ in range(1, depth):
            for k in range(KC):
                ada_pending.append((l_, k))
    if stop != 'precast':
        pro_x()
        fw.barrier()
    bg_pump(0, upto=0)
    if stop not in ('precast', 'xT'):
        pro_ada()
    fw.barrier()

    def make_h(l, c0, n, cond, hT, B_h, xc, B_xc, sq, B_sq, rstd, B_rstd, ss_bk, ctr):
        for f in range(KC):
            r = ctr[0] % len(xc)
            ctr[0] += 1
            fw.dma("sp", xc[r][:, 0:n], xT[f * 128:(f + 1) * 128, c0:c0 + n], reads=[B_xT], writes=[B_xc[r]])
            q = f % 2
            fw.op("act", lambda e, r=r, q=q: e.activation(out=sq[q][:, 0:n], in_=xc[r][:, 0:n], func=AF.Square),
                  reads=[B_xc[r]], writes=[B_sq[q]])
            fw.op("pe", lambda e, q=q, f=f: e.matmul(psum[ss_bk][:, 0:n], ones_f[:], sq[q][:, 0:n], start=(f == 0), stop=(f == KC - 1)),
                  reads=[B_sq[q], B_const], writes=[B_ps[ss_bk]])
        fw.op("act", lambda e: e.activation(out=rstd[:, 0:n], in_=psum[ss_bk][:, 0:n], func=AF.Sqrt, bias=epsb[:, 0:1], scale=1.0 / D),
              reads=[B_ps[ss_bk], B_const], writes=[B_rstd])
        fw.op("dve", lambda e: e.reciprocal(out=rstd[:, 0:n], in_=rstd[:, 0:n]), reads=[B_rstd], writes=[B_rstd])
        for f in range(KC):
            r = ctr[0] % len(xc)
            ctr[0] += 1
            fw.dma("sp", xc[r][:, 0:n], xT[f * 128:(f + 1) * 128, c0:c0 + n], reads=[B_xT], writes=[B_xc[r]])
            fw.op("dve", lambda e, r=r: e.tensor_tensor(out=xc[r][:, 0:n], in0=xc[r][:, 0:n], in1=rstd[:, 0:n], op=ALU.mult),
                  reads=[B_xc[r], B_rstd], writes=[B_xc[r]])
            fw.op("act", lambda e, r=r, f=f: e.activation(out=hT[:, f * 512:f * 512 + n], in_=xc[r][:, 0:n], func=AF.Identity,
                                                        bias=mod_ap(SHt, l, f, cond), scale=mod_ap(Gt, l, f, cond)),
                  reads=[B_xc[r], B_mod], writes=[B_h])

    def group_mm(bk, lhs_fn, rhs_fn, nk, n, reads, m=128):
        def mm(e):
            for k in range(nk):
                ins = e.matmul(psum[bk][0:m, 0:n], lhs_fn(k), rhs_fn(k), start=(k == 0), stop=(k == nk - 1))
            return ins
        fw.op("pe", mm, reads=reads, writes=[B_ps[bk]])

    def evac(eng, out, in_, reads, writes):
        if eng == "act":
            fw.op("act", lambda e: e.activation(out=out, in_=in_, func=AF.Copy), reads=reads, writes=writes)
        else:
            fw.op("dve", lambda e: e.tensor_copy(out=out, in_=in_), reads=reads, writes=writes)

    def out_transposed(src_fn, nfeat_chunks, fw_feat, ntok, dst_fn, stage, B_stage, bkc, reads):
        for ts in range(ntok // 128):
            b = ts % 2
            for c in range(nfeat_chunks):
                bk = bkc[0] % 8
                bkc[0] += 1
                fw.op("pe", lambda e, c=c, ts=ts, bk=bk: e.transpose(psum[bk][:, 0:fw_feat], src_fn(c)[:, ts * 128:(ts + 1) * 128], ident[0:fw_feat, 0:fw_feat]),
                      reads=reads + [B_const], writes=[B_ps[bk]])
                evac("dve" if c % 2 else "act", stage[b][:, c * fw_feat:(c + 1) * fw_feat], psum[bk][:, 0:fw_feat], [B_ps[bk]], [B_stage[b]])
            fw.dma("pool", dst_fn(ts), stage[b][:, 0:nfeat_chunks * fw_feat], reads=[B_stage[b]], is_output=True)

    def even_layer(l):
        i = l // 2
        bg_pump(0, upto=l)
        with ExitStack() as esL:
            ckvT = S(esL, "ckvT", [128, 4 * NKEY], BF16)
            kpeT = S(esL, "kpeT", [128, NKEY], BF16)
            B_ckvT = Buf("ckvT")
            B_kpeT = Buf("kpeT")
            fw.op("pool", lambda e: e.memset(kpeT[64:128, :], 0.0), writes=[B_kpeT])
            eo = i * 56
            convw = lambda f, tap: esm[:, eo + f * 3 + tap: eo + f * 3 + tap + 1]
            qg = lambda c: esm[:, eo + 48 + c: eo + 48 + c + 1]
            kg = lambda c: esm[:, eo + 52 + c: eo + 52 + c + 1]

            with ExitStack() as es:
                cin = [S(es, "cin%d" % j, [128, 576]) for j in range(2)]
                B_cin = [Buf(), Buf()]
                for kt in range(4):
                    b = kt % 2
                    fw.dma("sp", cin[b][:, 0:512], cckv[i, kt * 128:(kt + 1) * 128, :], writes=[B_cin[b]])
                    fw.dma("sp", cin[b][:, 512:576], ckpe[i, kt * 128:(kt + 1) * 128, :], writes=[B_cin[b]])
                    for c in range(4):
                        bk = (kt * 5 + c) % 8
                        fw.op("pe", lambda e, b=b, c=c, bk=bk: e.transpose(psum[bk][:, 0:128], cin[b][:, c * 128:(c + 1) * 128], ident[:]),
                              reads=[B_cin[b], B_const], writes=[B_ps[bk]])
                        evac("act" if c % 2 else "dve", ckvT[:, c * NKEY + kt * 128: c * NKEY + (kt + 1) * 128], psum[bk][:, 0:128], [B_ps[bk]], [B_ckvT])
                    bk = (kt * 5 + 4) % 8
                    fw.op("pe", lambda e, b=b, bk=bk: e.transpose(psum[bk][0:64, 0:128], cin[b][:, 512:576], ident[:]),
                          reads=[B_cin[b], B_const], writes=[B_ps[bk]])
                    evac("dve", kpeT[0:64, kt * 128:(kt + 1) * 128], psum[bk][0:64, 0:128], [B_ps[bk]], [B_kpeT])
            fw.barrier()

            with ExitStack() as es:
                xc = [S(es, "xc%d" % j, [128, 512]) for j in range(3)]
                B_xc = [Buf() for _ in xc]
                sq = [S(es, "sq%d" % j, [128, 512]) for j in range(2)]
                B_sq = [Buf(), Buf()]
                rstd = S(es, "rstd", [128, 512])
                B_rstd = Buf()
                hT = [S(es, "hT%d" % j, [128, KC * 512], BF16) for j in range(2)]
                B_h = [Buf(), Buf()]
                wp = [S(es, "wp%d" % j, [128, KC * 512], BF16) for j in range(3)]
                B_wp = [Buf(), Buf(), Buf()]
                ccs = S(es, "ccs", [128, 512]); pt = S(es, "pt", [128, 512]); sg = S(es, "sg", [128, 512])
                cbg = S(es, "cbg", [128, 512]); acc = S(es, "acc", [128, 512])
                B_ccs, B_pt, B_sg, B_cbg, B_acc = Buf(), Buf(), Buf(), Buf(), Buf()
                yo = [S(es, "yo%d" % j, [128, 512], BF16) for j in range(2)]
                B_yo = [Buf(), Buf()]
                raw = S(es, "raw", [128, 4 * 512]); B_raw = Buf()
                sqq = [S(es, "sqq%d" % j, [128, 512]) for j in range(2)]; B_sqq = [Buf(), Buf()]
                rq = S(es, "rq", [128, 512]); B_rq = Buf()
                nrm = S(es, "nrm", [128, 4 * 512]); B_nrm = Buf()
                qno = S(es, "qno", [128, 4 * 512], BF16); B_qno = Buf()
                rt = S(es, "rt", [64, 2 * 512]); B_rt = Buf()
                kr = S(es, "kr", [64, 2 * 512]); B_kr = Buf()
                stage = [S(es, "stg%d" % j, [128, 512]) for j in range(2)]; B_stage = [Buf(), Buf()]
                ctr = [0]
                bkc = [0]
                pc = [0]

                tiles = sample_tiles()
                tiles.append(dict(c0=NS, n=512, outs=[(0, 256, 0, 256, True, True), (256, 512, 256, 512, True, True)], cond=1, rope=False))
                wv = e_win_b[i].rearrange("(q p) c -> q p c", p=128)

                def ensure_panels(upto):
                    while pc[0] <= upto and pc[0] < len(tiles) * E_NP:
                        pn_ = pc[0] % E_NP
                        b_ = pc[0] % 3
                        pc[0] += 1
                        fw.dma("sp", wp[b_][:], wv[pn_], reads=[wbuf("e_win", i, pn_ * 128)], writes=[B_wp[b_]])
                        bg_pump(1)

                def nextbank():
                    bk = 1 + (bkc[0] % 6)
                    bkc[0] += 1
                    return bk

                make_h(l, tiles[0]["c0"], tiles[0]["n"], tiles[0]["cond"], hT[0], B_h[0], xc, B_xc, sq, B_sq, rstd, B_rstd, 0, ctr)
                for ti, tl in enumerate(tiles):
                    c0, n, cond = tl["c0"], tl["n"], tl["cond"]
                    hb = ti % 2
                    h = hT[hb]
                    if tl["rope"]:
                        fw.dma("sp", rt[:, 0:n], ropeT[0, :, c0:c0 + n], writes=[B_rt])
                        fw.dma("sp", rt[:, 512:512 + n], ropeT[1, :, c0:c0 + n], writes=[B_rt])
                    for pn in range(E_NP):
                        sidx = ti * E_NP + pn
                        ensure_panels(sidx + 2)
                        cur = sidx % 3
                        if pn == 10 and ti + 1 < len(tiles):
                            tn = tiles[ti + 1]
                            make_h(l, tn["c0"], tn["n"], tn["cond"], hT[1 - hb], B_h[1 - hb], xc, B_xc, sq, B_sq, rstd, B_rstd, 0, ctr)
                        w = wp[cur]
                        rd = [B_wp[cur], B_h[hb]]

                        def lhs(g, m0=0, m1=128):
                            return lambda k: w[:, k * 512 + g * 128 + m0: k * 512 + g * 128 + m1]
                        rhs = lambda k: h[:, k * 512:k * 512 + n]
                        if pn < 16:
                            f = pn
                            bcb, bcc, bcx, bcg = nextbank(), nextbank(), nextbank(), nextbank()
                            group_mm(bcc, lhs(1), rhs, KC, n, rd)
                            group_mm(bcx, lhs(2), rhs, KC, n, rd)
                            group_mm(bcg, lhs(3), rhs, KC, n, rd)
                            group_mm(bcb, lhs(0), rhs, KC, n, rd)
                            fw.op("act", lambda e, bcc=bcc: e.activation(out=ccs[:, 0:n], in_=psum[bcc][:, 0:n], func=AF.Copy),
                                  reads=[B_ps[bcc]], writes=[B_ccs])
                            fw.op("dve", lambda e, bcx=bcx: e.tensor_tensor(out=pt[:, 0:n], in0=psum[bcx][:, 0:n], in1=ccs[:, 0:n], op=ALU.mult),
                                  reads=[B_ps[bcx], B_ccs], writes=[B_pt])
                            fw.op("act", lambda e, bcg=bcg: e.activation(out=sg[:, 0:n], in_=psum[bcg][:, 0:n], func=AF.Silu),
                                  reads=[B_ps[bcg]], writes=[B_sg])
                            fw.op("dve", lambda e, bcb=bcb: e.tensor_tensor(out=cbg[:, 0:n], in0=psum[bcb][:, 0:n], in1=sg[:, 0:n], op=ALU.mult),
                                  reads=[B_ps[bcb], B_sg], writes=[B_cbg])
                            yb = f % 2
                            for (oa, ob, sa, sb, zl, zr) in tl["outs"]:
                                fw.op("act", lambda e, oa=oa, ob=ob, f=f: e.activation(out=acc[:, oa:ob], in_=pt[:, oa:ob], func=AF.Identity, scale=convw(f, 1)),
                                      reads=[B_pt, B_const], writes=[B_acc])
                                la = max(oa, sa + 1)
                                fw.op("dve", lambda e, la=la, ob=ob, f=f: e.scalar_tensor_tensor(out=acc[:, la:ob], in0=pt[:, la - 1:ob - 1], scalar=convw(f, 0),
                                                                                             in1=acc[:, la:ob], op0=ALU.mult, op1=ALU.add),
                                      reads=[B_pt, B_acc, B_const], writes=[B_acc])
                                rb = min(ob, sb - 1)
                                fw.op("dve", lambda e, oa=oa, rb=rb, f=f: e.scalar_tensor_tensor(out=acc[:, oa:rb], in0=pt[:, oa + 1:rb + 1], scalar=convw(f, 2),
                                                                                             in1=acc[:, oa:rb], op0=ALU.mult, op1=ALU.add),
                                      reads=[B_pt, B_acc, B_const], writes=[B_acc])
                                fw.op("dve", lambda e, oa=oa, ob=ob, yb=yb: e.tensor_tensor(out=yo[yb][:, oa:ob], in0=acc[:, oa:ob], in1=cbg[:, oa:ob], op=ALU.mult),
                                      reads=[B_acc, B_cbg], writes=[B_yo[yb]])
                                fw.dma("pool", ymixT[f * 128:(f + 1) * 128, c0 + oa:c0 + ob], yo[yb][:, oa:ob], reads=[B_yo[yb]], writes=[B_ymix])
                        elif pn in (16, 17):
                            ssb = 7
                            for c in range(4):
                                bk = nextbank()
                                group_mm(bk, lhs(c), rhs, KC, n, rd)
                                fw.op("act", lambda e, bk=bk, c=c: e.activation(out=raw[:, c * 512:c * 512 + n], in_=psum[bk][:, 0:n], func=AF.Copy),
                                      reads=[B_ps[bk]], writes=[B_raw])
                                q = c % 2
                                fw.op("act", lambda e, bk=bk, q=q: e.activation(out=sqq[q][:, 0:n], in_=psum[bk][:, 0:n], func=AF.Square),
                                      reads=[B_ps[bk]], writes=[B_sqq[q]])
                                fw.op("pe", lambda e, q=q, c=c: e.matmul(psum[ssb][:, 0:n], ones_f[:], sqq[q][:, 0:n], start=(c == 0), stop=(c == 3)),
                                      reads=[B_sqq[q], B_const], writes=[B_ps[ssb]])
                            fw.op("act", lambda e: e.activation(out=rq[:, 0:n], in_=psum[ssb][:, 0:n], func=AF.Sqrt, bias=epsb[:, 0:1], scale=1.0 / 512),
                                  reads=[B_ps[ssb], B_const], writes=[B_rq])
                            fw.op("dve", lambda e: e.reciprocal(out=rq[:, 0:n], in_=rq[:, 0:n]), reads=[B_rq], writes=[B_rq])
                            for c in range(4):
                                fw.op("dve", lambda e, c=c: e.tensor_tensor(out=raw[:, c * 512:c * 512 + n], in0=raw[:, c * 512:c * 512 + n], in1=rq[:, 0:n], op=ALU.mult),
                                      reads=[B_raw, B_rq], writes=[B_raw])
                                for (oa, ob, sa, sb, zl, zr) in tl["outs"]:
                                    if pn == 16:
                                        fw.op("act", lambda e, c=c, oa=oa, ob=ob: e.activation(out=qno[:, c * 512 + oa:c * 512 + ob], in_=raw[:, c * 512 + oa:c * 512 + ob],
                                                                                                func=AF.Identity, scale=qg(c)),
                                              reads=[B_raw, B_const], writes=[B_qno])
                                    else:
                                        kc0 = c * NKEY + 512 + c0
                                        fw.op("act", lambda e, c=c, oa=oa, ob=ob, kc0=kc0: e.activation(out=ckvT[:, kc0 + oa:kc0 + ob], in_=raw[:, c * 512 + oa:c * 512 + ob],
                                                                                                         func=AF.Identity, scale=kg(c)),
                                              reads=[B_raw, B_const], writes=[B_ckvT])
                                if pn == 17 and cond == 1:
                                    fw.op("act", lambda e, c=c: e.activation(out=nrm[:, c * 512:c * 512 + n], in_=raw[:, c * 512:c * 512 + n], func=AF.Identity, scale=kg(c)),
                                          reads=[B_raw, B_const], writes=[B_nrm])
                            if pn == 16:
                                for (oa, ob, sa, sb, zl, zr) in tl["outs"]:
                                    fw.dma("pool", qnT.rearrange("(c p) t -> p c t", p=128)[:, :, c0 + oa:c0 + ob],
                                           qno[:].rearrange("p (c t) -> p c t", c=4)[:, :, oa:ob], reads=[B_qno], writes=[B_qnT])
                            elif cond == 1:
                                for pr in range(2):
                                    out_transposed(lambda c, pr=pr: nrm[:, c * 512 + pr * 256:c * 512 + pr * 256 + 256], 4, 128, 256,
                                                   lambda ts, pr=pr: new_ckv[pr, i, ts * 128:(ts + 1) * 128, :], stage, B_stage, bkc, [B_nrm])
                        elif pn < 22:
                            for g in range(4):
                                f = (pn - 18) * 4 + g
                                bk = nextbank()
                                group_mm(bk, lhs(g), rhs, KC, n, rd)
                                yb = g % 2
                                fw.op("act", lambda e, bk=bk, yb=yb: e.activation(out=yo[yb][:, 0:n], in_=psum[bk][:, 0:n], func=AF.Silu),
                                      reads=[B_ps[bk]], writes=[B_yo[yb]])
                                for (oa, ob, sa, sb, zl, zr) in tl["outs"]:
                                    fw.dma("pool", smgT[f * 128:(f + 1) * 128, c0 + oa:c0 + ob], yo[yb][:, oa:ob], reads=[B_yo[yb]], writes=[B_smgT])
                        else:
                            bka, bkb = nextbank(), nextbank()
                            group_mm(bka, lhs(0, 0, 64), rhs, KC, n, rd, m=64)
                            kc0 = 512 + c0
                            if tl["rope"]:
                                group_mm(bkb, lhs(0, 64, 128), rhs, KC, n, rd, m=64)
                                fw.op("dve", lambda e, bka=bka: e.tensor_tensor(out=kr[:, 0:n], in0=psum[bka][0:64, 0:n], in1=rt[:, 0:n], op=ALU.mult),
                                      reads=[B_ps[bka], B_rt], writes=[B_kr])
                                fw.op("dve", lambda e, bkb=bkb: e.tensor_tensor(out=kr[:, 512:512 + n], in0=psum[bkb][0:64, 0:n], in1=rt[:, 512:512 + n], op=ALU.mult),
                                      reads=[B_ps[bkb], B_rt], writes=[B_kr])
                                for (oa, ob, sa, sb, zl, zr) in tl["outs"]:
                                    fw.op("dve", lambda e, oa=oa, ob=ob, kc0=kc0: e.tensor_tensor(out=kpeT[0:64, kc0 + oa:kc0 + ob], in0=kr[:, oa:ob], in1=kr[:, 512 + oa:512 + ob], op=ALU.add),
                                          reads=[B_kr], writes=[B_kpeT])
                            else:
                                fw.op("act", lambda e, bka=bka: e.activation(out=kr[:, 0:n], in_=psum[bka][0:64, 0:n], func=AF.Copy),
                                      reads=[B_ps[bka]], writes=[B_kr])
                                fw.op("dve", lambda e, kc0=kc0: e.tensor_copy(out=kpeT[0:64, kc0:kc0 + n], in_=kr[:, 0:n]), reads=[B_kr], writes=[B_kpeT])
                                for pr in range(2):
                                    out_transposed(lambda c, pr=pr: kr[:, pr * 256:pr * 256 + 256], 1, 64, 256,
                                                   lambda ts, pr=pr: new_kpe[pr, i, ts * 128:(ts + 1) * 128, :], stage, B_stage, bkc, [B_kr])
            fw.barrier()

            with ExitStack() as es:
                whd = [S(es, "whd%d" % j, [128, 4 * 512], BF16) for j in range(2)]; B_whd = [Buf(), Buf()]
                KnT = [S(es, "KnT%d" % j, [128, NKEY], BF16) for j in range(2)]; B_Kn = [Buf(), Buf()]
                Vh = [S(es, "Vh%d" % j, [128, 40 * 128], BF16) for j in range(2)]; B_V = [Buf(), Buf()]
                qn = [S(es, "qn%d" % j, [128, 4 * 512], BF16) for j in range(2)]; B_qn = [Buf(), Buf()]
                Qn = [S(es, "Qn%d" % j, [128, 512], BF16) for j in range(2)]; B_Qn = [Buf(), Buf()]
                Qr = [S(es, "Qr%d" % j, [128, 512], BF16) for j in range(2)]; B_Qr = [Buf(), Buf()]
                for j in range(2):
                    fw.op("pool", lambda e: e.memset(Qr[j][64:128, :], 0.0), writes=[B_Qr[j]])
                rt = [S(es, "rt%d" % j, [64, 2 * 512]) for j in range(2)]; B_rt = [Buf(), Buf()]
                t1 = S(es, "t1", [64, 512]); t2 = S(es, "t2", [64, 512]); B_t1 = Buf(); B_t2 = Buf()
                PT = [S(es, "PT%d" % j, [128, 1024], BF16) for j in range(4)]; B_PT = [Buf() for _ in range(4)]
                rl = S(es, "rl", [128, 512]); B_rl = Buf()
                yf = S(es, "yf", [128, 512]); B_yf = Buf()
                smg = [S(es, "smg%d" % j, [128, 512], BF16) for j in range(2)]; B_smg = [Buf(), Buf()]
                yo = [S(es, "yao%d" % j, [128, 512], BF16) for j in range(2)]; B_yo = [Buf(), Buf()]
                mc = [0]
                def misc():
                    bk = 7
                    mc[0] += 1
                    return bk
                whv = e_whd_b[i].rearrange("(h p) c -> h p c", p=128)
                seqs = [dict(t0=0, n=NS, k0=0, k1=4608, rope=True),
                        dict(t0=NS, n=256, k0=4608, k1=4864, rope=False),
                        dict(t0=NS + 256, n=256, k0=4864, k1=5120, rope=False)]
                qtc = [0]
                ptc = [0]
                sc_ = [0]
                accA = S(es, "accA", [128, 1024]); accB = S(es, "accB", [128, 1024]); B_accA = Buf(); B_accB = Buf()
                if ada_pending:
                    wa_e = [S(es, "wae%d" % j, [128, 3 * D]) for j in range(2)]; B_wa_e = [Buf(), Buf()]
                ada_n = [0]
                ada_loaded = []

                def ada_pump():
                    if ada_loaded:
                        l_, k_, bi = ada_loaded.pop(0)
                        ada_kstep(l_, k_, wa_e[bi], B_wa_e[bi], 7, load=False)
                    if ada_pending:
                        l_, k_ = ada_pending.pop(0)
                        bi = ada_n[0] % 2
                        ada_n[0] += 1
                        ada_load(l_, k_, wa_e[bi], B_wa_e[bi])
                        ada_loaded.append((l_, k_, bi))
                ada_pump()
                items = []
                for sq_ in seqs:
                    QN_ = min(512, sq_["n"])
                    for qt in range(sq_["n"] // QN_):
                        items.append((sq_, qt, QN_))
                fw.dma("sp", whd[0][:], whv[0], reads=[wbuf("e_whd", i, 0)], writes=[B_whd[0]])

                def kv_proj(hd):
                    hb = hd % 2
                    w = whd[hb]
                    for ct in range(NKEY // 512):
                        bk = misc()
                        group_mm(bk, lambda k: w[:, k * 512 + 256:k * 512 + 384], lambda k: ckvT[:, k * NKEY + ct * 512:k * NKEY + (ct + 1) * 512],
                                 4, 512, [B_whd[hb], B_ckvT])
                        evac("act" if ct % 2 else "dve", KnT[hb][:, ct * 512:(ct + 1) * 512], psum[bk][:, 0:512], [B_ps[bk]], [B_Kn[hb]])
                    for g4 in range(10):
                        bk = misc()

                        def vmm(e):
                            for j in range(4):
                                kt = g4 * 4 + j
                                for k in range(4):
                                    ins = e.matmul(psum[bk][:, j * 128:(j + 1) * 128], ckvT[:, k * NKEY + kt * 128:k * NKEY + (kt + 1) * 128],
                                                   w[:, k * 512 + 384:k * 512 + 512], start=(k == 0), stop=(k == 3))
                            return ins
                        fw.op("pe", vmm, reads=[B_whd[hb], B_ckvT], writes=[B_ps[bk]])
                        evac("dve" if g4 % 2 else "act", Vh[hb][:, g4 * 512:(g4 + 1) * 512], psum[bk][:, 0:512], [B_ps[bk]], [B_V[hb]])

                def q_proj(hd, item, qb):
                    sq_, qt, QN = item
                    hb = hd % 2
                    w = whd[hb]
                    tq = sq_["t0"] + qt * QN
                    fw.dma("sp", qn[qb][:].rearrange("p (c t) -> p c t", c=4)[:, :, 0:QN],
                           qnT.rearrange("(c p) t -> p c t", p=128)[:, :, tq:tq + QN], reads=[B_qnT], writes=[B_qn[qb]])
                    fw.dma("sp", smg[qb][:, 0:QN], smgT[hd * 128:(hd + 1) * 128, tq:tq + QN], reads=[B_smgT], writes=[B_smg[qb]])
                    if sq_["rope"]:
                        fw.dma("sp", rt[qb][:, 0:QN], ropeT[0, :, tq:tq + QN], writes=[B_rt[qb]])
                        fw.dma("sp", rt[qb][:, 512:512 + QN], ropeT[1, :, tq:tq + QN], writes=[B_rt[qb]])
                    rhsq = lambda k: qn[qb][:, k * 512:k * 512 + QN]
                    bk = misc()
                    group_mm(bk, lambda k: w[:, k * 512:k * 512 + 128], rhsq, 4, QN, [B_whd[hb], B_qn[qb]])
                    evac("act", Qn[qb][:, 0:QN], psum[bk][:, 0:QN], [B_ps[bk]], [B_Qn[qb]])
                    bka = misc()
                    group_mm(bka, lambda k: w[:, k * 512 + 128:k * 512 + 192], rhsq, 4, QN, [B_whd[hb], B_qn[qb]], m=64)
                    if sq_["rope"]:
                        fw.op("dve", lambda e: e.tensor_tensor(out=t1[:, 0:QN], in0=psum[bka][0:64, 0:QN], in1=rt[qb][:, 0:QN], op=ALU.mult),
                              reads=[B_ps[bka], B_rt[qb]], writes=[B_t1])
                        bkb = misc()
                        group_mm(bkb, lambda k: w[:, k * 512 + 192:k * 512 + 256], rhsq, 4, QN, [B_whd[hb], B_qn[qb]], m=64)
                        fw.op("dve", lambda e: e.tensor_tensor(out=t2[:, 0:QN], in0=psum[bkb][0:64, 0:QN], in1=rt[qb][:, 512:512 + QN], op=ALU.mult),
                              reads=[B_ps[bkb], B_rt[qb]], writes=[B_t2])
                        fw.op("dve", lambda e: e.tensor_tensor(out=Qr[qb][0:64, 0:QN], in0=t1[:, 0:QN], in1=t2[:, 0:QN], op=ALU.add),
                              reads=[B_t1, B_t2], writes=[B_Qr[qb]])
                    else:
                        evac("dve", Qr[qb][0:64, 0:QN], psum[bka][0:64, 0:QN], [B_ps[bka]], [B_Qr[qb]])

                def attend(hd, item, qb, hook):
                    sq_, qt, QN = item
                    hb = hd % 2
                    tq = sq_["t0"] + qt * QN
                    nkt = (sq_["k1"] - sq_["k0"]) // 128
                    npair = nkt // 2
                    hookpi = max(npair // 2 - 1, 0)
                    v3 = lambda ap: ap.rearrange("p (a b) -> p a b", a=2)[:, :, 0:QN]

                    def s_pair(pi):
                        j = sc_[0] % 3
                        sc_[0] += 1
                        Bp = [B_ps[2 * j], B_ps[2 * j + 1]]

                        def mm(e):
                            for hf in range(2):
                                kc = sq_["k0"] + (2 * pi + hf) * 128
                                e.matmul(psum[2 * j + hf][:, 0:QN], KnT[hb][:, kc:kc + 128], Qn[qb][:, 0:QN], start=True, stop=False)
                                ins = e.matmul(psum[2 * j + hf][:, 0:QN], kpeT[:, kc:kc + 128], Qr[qb][:, 0:QN], start=False, stop=True)
                            return ins
                        fw.op("pe", mm, reads=[B_Kn[hb], B_Qn[qb], B_kpeT, B_Qr[qb]], writes=Bp)
                        pb = ptc[0] % 4
                        ptc[0] += 1
                        for hf in range(2):
                            fw.op("act", lambda e: e.activation(out=PT[pb][:, hf * 512:hf * 512 + QN], in_=psum[2 * j + hf][:, 0:QN], func=AF.Exp, scale=SCALE),
                                  reads=[Bp[hf]], writes=[B_PT[pb]])
                        if pi > hookpi:
                            return pb
                        eng, acc_, B_acc_ = ("dve", accA, B_accA) if pi % 2 == 0 else ("pool", accB, B_accB)
                        if pi < 2:
                            fw.op(eng, lambda e: e.tensor_copy(out=v3(acc_[:]), in_=v3(PT[pb][:])), reads=[B_PT[pb]], writes=[B_acc_])
                        else:
                            fw.op(eng, lambda e: e.tensor_tensor(out=v3(acc_[:]), in0=v3(acc_[:]), in1=v3(PT[pb][:]), op=ALU.add),
                                  reads=[B_PT[pb], B_acc_], writes=[B_acc_])
                        return pb

                    def pv_pair(pi, pb):
                        def mm(e):
                            for hf in range(2):
                                kt = 2 * pi + hf
                                vk = (sq_["k0"] // 128 + kt) * 128
                                ins = e.matmul(psum[6][:, 0:QN], Vh[hb][:, vk:vk + 128], PT[pb][:, hf * 512:hf * 512 + QN], start=(kt == 0), stop=(kt == nkt - 1))
                            if pi > hookpi:
                                for hf in range(2):
                                    ins = e.matmul(psum[7][:, 0:QN], ones_b[:], PT[pb][:, hf * 512:hf * 512 + QN],
                                                   start=(pi == hookpi + 1 and hf == 0), stop=False)
                            return ins
                        fw.op("pe", mm, reads=[B_V[hb], B_PT[pb], B_const], writes=[B_ps[6]] + ([B_ps[7]] if pi > hookpi else []))
                    pend = [s_pair(0)]
                    if npair > 1:
                        pend.append(s_pair(1))
                    for pi in range(npair):
                        if pi + 2 < npair:
                            pend.append(s_pair(pi + 2))
                        pv_pair(pi, pend[pi])
                        if pi == hookpi and hook is not None:
                            hook()
                    fw.op("dve", lambda e: e.tensor_tensor(out=yf[:, 0:QN], in0=psum[6][:, 0:QN], in1=smg[qb][:, 0:QN], op=ALU.mult),
                          reads=[B_ps[6], B_smg[qb]], writes=[B_yf])
                    fw.op("dve", lambda e: e.tensor_tensor(out=accA[:, 0:QN], in0=accA[:, 0:QN], in1=accA[:, 512:512 + QN], op=ALU.add),
                          reads=[B_accA], writes=[B_accA])
                    if hookpi >= 1:
                        fw.op("pool", lambda e: e.tensor_tensor(out=accB[:, 0:QN], in0=accB[:, 0:QN], in1=accB[:, 512:512 + QN], op=ALU.add),
                              reads=[B_accB], writes=[B_accB])
                        fw.op("dve", lambda e: e.tensor_tensor(out=accA[:, 0:QN], in0=accA[:, 0:QN], in1=accB[:, 0:QN], op=ALU.add),
                              reads=[B_accA, B_accB], writes=[B_accA])
                    fw.op("pe", lambda e: e.matmul(psum[7][:, 0:QN], ones_f[:], accA[:, 0:QN], start=(npair - 1 <= hookpi), stop=True),
                          reads=[B_const, B_accA], writes=[B_ps[7]])
                    fw.op("dve", lambda e: e.reciprocal(out=rl[:, 0:QN], in_=psum[7][:, 0:QN]), reads=[B_ps[7]], writes=[B_rl])
                    fw.op("dve", lambda e: e.tensor_tensor(out=yo[qb][:, 0:QN], in0=yf[:, 0:QN], in1=rl[:, 0:QN], op=ALU.mult),
                          reads=[B_yf, B_rl], writes=[B_yo[qb]])
                    fw.dma("pool", ymixT[D + hd * 128:D + (hd + 1) * 128, tq:tq + QN], yo[qb][:, 0:QN], reads=[B_yo[qb]], writes=[B_ymix])

                kv_proj(0)
                q_proj(0, items[0], 0)
                qcount = 0
                for hd in range(16):
                    if hd + 1 < 16:
                        fw.dma("sp", whd[(hd + 1) % 2][:], whv[hd + 1], reads=[wbuf("e_whd", i, (hd + 1) * 128)], writes=[B_whd[(hd + 1) % 2]])
                    for ii, item in enumerate(items):
                        qb = qcount % 2
                        qcount += 1

                        def hook(hd=hd, ii=ii, qb=qb):
                            ada_pump()
                            if ii == 4 and hd + 1 < 16:
                                kv_proj(hd + 1)
                            if ii + 1 < len(items):
                                q_proj(hd, items[ii + 1], 1 - qb)
                            elif hd + 1 < 16:
                                q_proj(hd + 1, items[0], 1 - qb)
                        attend(hd, item, qb, hook)
                while ada_pending or ada_loaded:
                    ada_pump()
            fw.barrier()
        out_proj(l, e_wout_b[l // 2], "e_wout", l // 2)
        fw.barrier()

    def out_proj(l, wsrc, wname, wi):
        with ExitStack() as es:
            ym = [S(es, "ym%d" % j, [128, 32 * 512], BF16) for j in range(2)]; B_ym = [Buf(), Buf()]
            wo = [S(es, "wo%d" % j, [128, 32 * 256], BF16) for j in range(3)]; B_wo = [Buf(), Buf(), Buf()]
            xc = [S(es, "xo%d" % j, [128, 512]) for j in range(3)]; B_xc = [Buf() for _ in xc]
            wv = wsrc.rearrange("(q p) c -> q p c", p=128)
            yv = ymixT.rearrange("(c p) t -> p c t", p=128)
            pc = [0]; xcn = [0]; bkc = [0]
            for ti in range(T // 512):
                c0 = ti * 512
                cond = 0 if c0 < NS else 1
                yb = ti % 2
                for part in range(4):
                    fw.dma("sp", ym[yb][:].rearrange("p (c t) -> p c t", c=32)[:, part * 8:(part + 1) * 8, :], yv[:, part * 8:(part + 1) * 8, c0:c0 + 512],
                           reads=[B_ymix], writes=[B_ym[yb]])
                for pn in range(8):
                    sidx = ti * 8 + pn
                    while pc[0] <= sidx + 2 and pc[0] < (T // 512) * 8:
                        fw.dma("sp", wo[pc[0] % 3][:], wv[pc[0] % 8], reads=[wbuf(wname, wi, (pc[0] % 8) * 128)], writes=[B_wo[pc[0] % 3]])
                        pc[0] += 1
                        bg_pump(1)
                    b = sidx % 3
                    for jj in range(2):
                        j = pn * 2 + jj
                        r = xcn[0] % 3
                        xcn[0] += 1
                        fw.dma("sp", xc[r][:], xT[j * 128:(j + 1) * 128, c0:c0 + 512], reads=[B_xT], writes=[B_xc[r]])
                        bk = bkc[0] % 4
                        bkc[0] += 1
                        group_mm(bk, lambda k, b=b, jj=jj: wo[b][:, k * 256 + jj * 128:k * 256 + (jj + 1) * 128],
                                 lambda k, yb=yb: ym[yb][:, k * 512:(k + 1) * 512], 32, 512, [B_wo[b], B_ym[yb]])
                        fw.op("dve", lambda e, bk=bk, r=r, j=j, cond=cond: e.scalar_tensor_tensor(out=xc[r][:], in0=psum[bk][:, 0:512], scalar=mod_ap(GAt, l, j, cond),
                                                                                              in1=xc[r][:], op0=ALU.mult, op1=ALU.add),
                              reads=[B_ps[bk], B_xc[r], B_mod], writes=[B_xc[r]])
                        fw.dma("pool", xT[j * 128:(j + 1) * 128, c0:c0 + 512], xc[r][:], reads=[B_xc[r]], writes=[B_xT])

    def odd_layer(l):
        i = l // 2
        bg_pump(0, upto=l)
        oo = i * 64
        lng = lambda fc: osm[:, oo + fc:oo + fc + 1]
        lnb = lambda fc: osm[:, oo + 32 + fc:oo + 32 + fc + 1]
        with ExitStack() as es:
            xc = [S(es, "xc%d" % j, [128, 512]) for j in range(3)]; B_xc = [Buf() for _ in xc]
            sq = [S(es, "sq%d" % j, [128, 512]) for j in range(2)]; B_sq = [Buf(), Buf()]
            rstd = S(es, "rstd", [128, 512]); B_rstd = Buf()
            hT = [S(es, "hT%d" % j, [128, KC * 512], BF16) for j in range(1)]; B_h = [Buf()]
            wp = [S(es, "wp%d" % j, [128, KC * 512], BF16) for j in range(2)]; B_wp = [Buf(), Buf()]
            vb = S(es, "vb", [128, 4 * 4096], BF16); B_vb = Buf()
            wsf = S(es, "wsf", [128, 2048]); wsb = S(es, "wsb", [128, 2048], BF16); B_ws = Buf()
            wsc = S(es, "wsc", [128, 4 * 2048], BF16); B_wsc = Buf()
            nmr = S(es, "nmr", [128, 4 * 128], BF16); B_nmr = Buf()
            st_s = S(es, "st_s", [128, 4 * 8]); st_q = S(es, "st_q", [128, 4 * 8]); B_st = Buf()
            junk = S(es, "junk", [128, 512], BF16); B_junk = Buf()
            mu = S(es, "mu", [128, 16]); B_mu = Buf()
            Cc = S(es, "Cc", [128, 32 * 128]); B_Cc = Buf()
            yT = S(es, "yT", [128, 32 * 512], BF16); B_yT = Buf()
            sgt = S(es, "sgt", [128, 512]); ugt = [S(es, "ugt%d" % j, [128, 512]) for j in range(2)]; tmp = S(es, "tmp", [128, 512])
            B_sgt = Buf(); B_ugt = [Buf(), Buf()]; B_tmp = Buf()
            es_setup = ExitStack()
            bsb = S(es_setup, "bsb", [128, 2048]); rsb = S(es_setup, "rsb", [128, 2048])
            ctr = [0]; pc = [0]; bkc = [0]; woc = [0]

            fw.dma("sp", wsf[:], o_wsT[i], writes=[B_ws])
            fw.dma("sp", bsb[:], o_bs[i].partition_broadcast(128), writes=[B_ws])
            fw.op("dve", lambda e: e.tensor_copy(out=wsb[:], in_=wsf[:]), reads=[B_ws], writes=[B_ws])
            for q4 in range(4):
                fw.op("pe", lambda e, q4=q4: e.matmul(psum[q4][:, 0:512], ones_f[:], wsf[:, q4 * 512:(q4 + 1) * 512], start=True, stop=True),
                      reads=[B_ws, B_const], writes=[B_ps[q4]])
                fw.op("dve", lambda e, q4=q4: e.tensor_copy(out=rsb[:, q4 * 512:(q4 + 1) * 512], in_=psum[q4][:, 0:512]), reads=[B_ps[q4]], writes=[B_ws])
            for fc in range(32):
                g = fc // 2
                fw.op("dve", lambda e, fc=fc, g=g: e.scalar_tensor_tensor(out=Cc[:, fc * 128:(fc + 1) * 128], in0=rsb[:, g * 128:(g + 1) * 128], scalar=lnb(fc),
                                                                       in1=bsb[:, g * 128:(g + 1) * 128], op0=ALU.mult, op1=ALU.add),
                      reads=[B_ws, B_const], writes=[B_Cc])

            fw.barrier()
            es_setup.close()
            wo = [S(es, "wo%d" % j, [128, 32 * 128], BF16) for j in range(3)]; B_wo = [Buf(), Buf(), Buf()]
            wv = o_win_b[i].rearrange("(q p) c -> q p c", p=128)
            wov = o_wout_b[i].rearrange("(q p) c -> q p c", p=128)

            def load_panel(pn):
                b = pc[0] % 2
                pc[0] += 1
                fw.dma("sp", wp[b][:], wv[pn], reads=[wbuf("o_win", i, pn * 128)], writes=[B_wp[b]])
                bg_pump(1)
                return b

            def nextbank():
                bk = 1 + (bkc[0] % 5)
                bkc[0] += 1
                return bk

            for ti in range(T // 512):
                c0 = ti * 512
                cond = 0 if c0 < NS else 1
                n = 512
                hb = 0
                make_h(l, c0, n, cond, hT[0], B_h[0], xc, B_xc, sq, B_sq, rstd, B_rstd, 0, ctr)
                h = hT[hb]
                wb = load_panel(0)
                for pn in range(O_NP):
                    cur = wb
                    if pn + 1 < O_NP:
                        wb = load_panel(pn + 1)
                    w = wp[cur]
                    if pn < 8:
                        for tc in range(4):
                            bk = nextbank()
                            group_mm(bk, lambda k, tc=tc: h[:, k * 512 + tc * 128:k * 512 + (tc + 1) * 128], lambda k: w[:, k * 512:(k + 1) * 512],
                                     KC, 512, [B_wp[cur], B_h[hb]])
                            col = tc * 8 + pn
                            fw.op("act", lambda e, bk=bk, tc=tc, pn=pn, col=col: e.activation(out=vb[:, tc * 4096 + pn * 512:tc * 4096 + (pn + 1) * 512], in_=psum[bk][:, 0:512],
                                                                                         func=AF.Copy, accum_out=st_s[:, col:col + 1]),
                                  reads=[B_ps[bk]], writes=[B_vb, B_st])
                            fw.op("act", lambda e, bk=bk, col=col: e.activation(out=junk[:], in_=psum[bk][:, 0:512], func=AF.Square, accum_out=st_q[:, col:col + 1]),
                                  reads=[B_ps[bk]], writes=[B_junk, B_st])
                        if pn == 7:
                            fw.op("dve", lambda e: e.tensor_reduce(out=mu[:, 0:4], in_=st_s[:].rearrange("p (t c) -> p t c", c=8), axis=mybir.AxisListType.X, op=ALU.add),
                                  reads=[B_st], writes=[B_mu])
                            fw.op("dve", lambda e: e.tensor_reduce(out=mu[:, 4:8], in_=st_q[:].rearrange("p (t c) -> p t c", c=8), axis=mybir.AxisListType.X, op=ALU.add),
                                  reads=[B_st], writes=[B_mu])
                            fw.op("dve", lambda e: e.tensor_scalar(out=mu[:, 0:8], in0=mu[:, 0:8], scalar1=1.0 / 4096, scalar2=None, op0=ALU.mult),
                                  reads=[B_mu], writes=[B_mu])
                            fw.op("dve", lambda e: e.tensor_tensor(out=mu[:, 8:12], in0=mu[:, 0:4], in1=mu[:, 0:4], op=ALU.mult), reads=[B_mu], writes=[B_mu])
                            fw.op("dve", lambda e: e.tensor_tensor(out=mu[:, 4:8], in0=mu[:, 4:8], in1=mu[:, 8:12], op=ALU.subtract), reads=[B_mu], writes=[B_mu])
                            fw.op("act", lambda e: e.activation(out=mu[:, 4:8], in_=mu[:, 4:8], func=AF.Sqrt, bias=epsb[:, 0:1], scale=1.0), reads=[B_mu, B_const], writes=[B_mu])
                            fw.op("dve", lambda e: e.reciprocal(out=mu[:, 4:8], in_=mu[:, 4:8]), reads=[B_mu], writes=[B_mu])
                            fw.op("dve", lambda e: e.scalar_tensor_tensor(out=mu[:, 8:12], in0=mu[:, 0:4], scalar=-1.0, in1=mu[:, 4:8], op0=ALU.mult, op1=ALU.mult),
                                  reads=[B_mu], writes=[B_mu])
                            for tc in range(4):
                                fw.op("dve", lambda e, tc=tc: e.tensor_scalar(out=wsc[:, tc * 2048:(tc + 1) * 2048], in0=wsf[:], scalar1=mu[:, 4 + tc:5 + tc], scalar2=None, op0=ALU.mult),
                                      reads=[B_ws, B_mu], writes=[B_wsc])
                                fw.op("dve", lambda e, tc=tc: e.tensor_scalar(out=nmr[:, tc * 128:(tc + 1) * 128], in0=ones_f[:], scalar1=mu[:, 8 + tc:9 + tc], scalar2=None, op0=ALU.mult),
                                      reads=[B_const, B_mu], writes=[B_nmr])
                    else:
                        for half in range(2):
                            fc = (pn - 8) * 2 + half
                            g = fc // 2
                            bu, bg = nextbank(), nextbank()
                            rhs = lambda k: h[:, k * 512:(k + 1) * 512]
                            group_mm(bg, lambda k, half=half: w[:, k * 512 + half * 256 + 128:k * 512 + half * 256 + 256], rhs, KC, 512, [B_wp[cur], B_h[hb]])
                            group_mm(bu, lambda k, half=half: w[:, k * 512 + half * 256:k * 512 + half * 256 + 128], rhs, KC, 512, [B_wp[cur], B_h[hb]])
                            gb = 6 + fc % 2

                            def gate_mm(e, fc=fc, g=g, gb=gb):
                                for tc in range(4):
                                    e.matmul(psum[gb][:, tc * 128:(tc + 1) * 128], vb[:, tc * 4096 + fc * 128:tc * 4096 + (fc + 1) * 128],
                                             wsc[:, tc * 2048 + g * 128:tc * 2048 + (g + 1) * 128], start=True, stop=False)
                                    ins = e.matmul(psum[gb][:, tc * 128:(tc + 1) * 128], nmr[:, tc * 128:(tc + 1) * 128],
                                                   wsb[:, g * 128:(g + 1) * 128], start=False, stop=True)
                                return ins
                            fw.op("pe", gate_mm, reads=[B_vb, B_wsc, B_nmr, B_ws], writes=[B_ps[gb]])
                            fw.op("act", lambda e, bg=bg: e.activation(out=sgt[:], in_=psum[bg][:, 0:512], func=AF.Silu), reads=[B_ps[bg]], writes=[B_sgt])
                            ub = fc % 2
                            fw.op("dve", lambda e, bu=bu, ub=ub: e.tensor_tensor(out=ugt[ub][:], in0=psum[bu][:, 0:512], in1=sgt[:], op=ALU.mult),
                                  reads=[B_ps[bu], B_sgt], writes=[B_ugt[ub]])
                            cbc = Cc[:, fc * 128:(fc + 1) * 128].unsqueeze(1).to_broadcast([128, 4, 128])
                            fw.op("dve", lambda e, gb=gb, fc=fc, cbc=cbc: e.scalar_tensor_tensor(out=tmp[:].rearrange("p (a b) -> p a b", a=4), in0=psum[gb][:, 0:512].rearrange("p (a b) -> p a b", a=4),
                                                                                          scalar=lng(fc), in1=cbc, op0=ALU.mult, op1=ALU.add),
                                  reads=[B_ps[gb], B_Cc, B_const], writes=[B_tmp])
                            fw.op("dve", lambda e, fc=fc, ub=ub: e.tensor_tensor(out=yT[:, fc * 512:(fc + 1) * 512], in0=tmp[:], in1=ugt[ub][:], op=ALU.mult),
                                  reads=[B_tmp, B_ugt[ub]], writes=[B_yT])
                for j in range(KC):
                    b = woc[0] % 3
                    woc[0] += 1
                    fw.dma("sp", wo[b][:], wov[j], reads=[wbuf("o_wout", i, j * 128)], writes=[B_wo[b]])
                    r = ctr[0] % len(xc)
                    ctr[0] += 1
                    fw.dma("sp", xc[r][:], xT[j * 128:(j + 1) * 128, c0:c0 + 512], reads=[B_xT], writes=[B_xc[r]])
                    bk = nextbank()
                    group_mm(bk, lambda k, b=b: wo[b][:, k * 128:(k + 1) * 128], lambda k: yT[:, k * 512:(k + 1) * 512], 32, 512, [B_wo[b], B_yT])
                    fw.op("dve", lambda e, bk=bk, r=r, j=j, cond=cond: e.scalar_tensor_tensor(out=xc[r][:], in0=psum[bk][:, 0:512], scalar=mod_ap(GAt, l, j, cond),
                                                                                          in1=xc[r][:], op0=ALU.mult, op1=ALU.add),
                          reads=[B_ps[bk], B_xc[r], B_mod], writes=[B_xc[r]])
                    fw.dma("pool", xT[j * 128:(j + 1) * 128, c0:c0 + 512], xc[r][:], reads=[B_xc[r]], writes=[B_xT])
        fw.barrier()

    for l in range(depth if stop is None else 0):
        if l % 2 == 0:
            even_layer(l)
        else:
            odd_layer(l)

    with ExitStack() as es:
      if stop in (None, 'ada', 'final'):
          xc = [S(es, "xc%d" % j, [128, 512]) for j in range(3)]; B_xc = [Buf() for _ in xc]
          sq = [S(es, "sq%d" % j, [128, 512]) for j in range(2)]; B_sq = [Buf(), Buf()]
          rstd = S(es, "rstd", [128, 512]); B_rstd = Buf()
          ystg_t = [S(es, "ystg%d" % j, [128, D]) for j in range(4)]; B_ystg = [Buf() for _ in range(4)]
          ctr = [0]; bkc = [0]; sc_ = [0]
          for ti in range(T // 512):
              c0 = ti * 512
              for f in range(KC):
                  r = ctr[0] % 3
                  ctr[0] += 1
                  fw.dma("sp", xc[r][:], xT[f * 128:(f + 1) * 128, c0:c0 + 512], reads=[B_xT], writes=[B_xc[r]])
                  q = f % 2
                  fw.op("act", lambda e, r=r, q=q: e.activation(out=sq[q][:], in_=xc[r][:], func=AF.Square), reads=[B_xc[r]], writes=[B_sq[q]])
                  fw.op("pe", lambda e, q=q, f=f: e.matmul(psum[0][:, 0:512], ones_f[:], sq[q][:], start=(f == 0), stop=(f == KC - 1)),
                        reads=[B_sq[q], B_const], writes=[B_ps[0]])
              fw.op("act", lambda e: e.activation(out=rstd[:], in_=psum[0][:, 0:512], func=AF.Sqrt, bias=epsb[:, 0:1], scale=1.0 / D),
                    reads=[B_ps[0], B_const], writes=[B_rstd])
              fw.op("dve", lambda e: e.reciprocal(out=rstd[:], in_=rstd[:]), reads=[B_rstd], writes=[B_rstd])
              for f in range(KC):
                  r = ctr[0] % 3
                  ctr[0] += 1
                  fw.dma("sp", xc[r][:], xT[f * 128:(f + 1) * 128, c0:c0 + 512], reads=[B_xT], writes=[B_xc[r]])
                  fw.op("dve", lambda e, r=r, f=f: e.scalar_tensor_tensor(out=xc[r][:], in0=xc[r][:], scalar=ngs[:, 4 * KC + f:4 * KC + f + 1], in1=rstd[:], op0=ALU.mult, op1=ALU.mult),
                        reads=[B_xc[r], B_rstd, B_const], writes=[B_xc[r]])
                  for ts in range(4):
                      bk = 1 + bkc[0] % 7
                      bkc[0] += 1
                      fw.op("pe", lambda e, r=r, ts=ts, bk=bk: e.transpose(psum[bk][:, 0:128], xc[r][:, ts * 128:(ts + 1) * 128], ident[:]),
                            reads=[B_xc[r], B_const], writes=[B_ps[bk]])
                      sb = ystg_t[ts]
                      evac("act" if ts % 2 else "dve", sb[:, f * 128:(f + 1) * 128], psum[bk][:, 0:128], [B_ps[bk]], [B_ystg[ts]])
              for ts in range(4):
                  fw.dma("pool", y_tok[c0 + ts * 128:c0 + (ts + 1) * 128, :], ystg_t[ts][:], reads=[B_ystg[ts]], is_output=True)

    seen = fw.seen["pool"]
    for k, v in fw.out_toks:
        if seen.get(k, 0) < v:
            seen[k] = v
            nc.gpsimd.wait_ge(fw.sem[k], v)
    top.close()
    return nc


ROPE_PERM = np.concatenate([np.arange(16, 32), np.arange(0, 16), np.arange(48, 64), np.arange(32, 48)])
ROPE_SIGN = np.concatenate([-np.ones(16), np.ones(16), -np.ones(16), np.ones(16)]).astype(np.float32)


def _rope_tables():
    rows = NS // 64
    row = np.repeat(np.arange(rows, dtype=np.float32), 64)
    col = np.tile(np.arange(64, dtype=np.float32), rows)
    inv = (10000.0 ** (-np.arange(16, dtype=np.float32) / 16)).astype(np.float32)
    ar = row[:, None] * inv
    ac = col[:, None] * inv
    ang = np.concatenate([ar, ar, ac, ac], axis=-1)
    cos = np.cos(ang).astype(np.float32)
    sin = np.sin(ang).astype(np.float32)
    return np.ascontiguousarray(np.stack([cos.T, (sin * ROPE_SIGN[None, :]).T]))


def _pk(v, nchunk):
    return np.ascontiguousarray(v.reshape(nchunk, 128).T)


def _tile_w(w, cols_per_panel):
    K = w.shape[0]
    kc = K // 128
    W = len(cols_per_panel[0])
    out = np.zeros((len(cols_per_panel), 128, kc, W), dtype=np.float32)
    w3 = w.reshape(kc, 128, w.shape[1])
    for q, cols in enumerate(cols_per_panel):
        cols = np.asarray(cols)
        valid = cols >= 0
        out[q][:, :, valid] = w3[:, :, cols[valid]].transpose(1, 0, 2)
    return out.reshape(len(cols_per_panel) * 128, kc * W)


def _even_cols():
    CW = 2048
    panels = []
    for f in range(16):
        panels.append(np.concatenate([g * CW + f * 128 + np.arange(128) for g in range(4)]))
    panels.append(8192 + np.arange(512))
    panels.append(8704 + np.arange(512))
    for m in range(4):
        panels.append(9280 + m * 512 + np.arange(512))
    panels.append(np.concatenate([9216 + np.arange(64), 9216 + ROPE_PERM, -np.ones(384, dtype=np.int64)]))
    return panels


def _odd_cols():
    panels = [4096 + pn * 512 + np.arange(512) for pn in range(8)]
    for q in range(16):
        f0, f1 = 2 * q, 2 * q + 1
        panels.append(np.concatenate([f0 * 128 + np.arange(128), 8192 + f0 * 128 + np.arange(128),
                                      f1 * 128 + np.arange(128), 8192 + f1 * 128 + np.arange(128)]))
    return panels


def _head_w(wqb, wkvb):
    out = np.zeros((16, 128, 4, 512), dtype=np.float32)
    q3 = wqb.reshape(4, 128, 3072)
    k3 = wkvb.reshape(4, 128, 4096)
    for h in range(16):
        out[h][:, :, 0:192] = q3[:, :, h * 192:(h + 1) * 192].transpose(1, 0, 2)
        out[h][:, :, 192:256] = q3[:, :, h * 192 + 128 + ROPE_PERM].transpose(1, 0, 2)
        out[h][:, :, 256:512] = k3[:, :, h * 256:(h + 1) * 256].transpose(1, 0, 2)
    return out.reshape(16 * 128, 4 * 512)


_NC_CACHE = {}


def kernel(x_prompt, x_sample, cache_ckv, cache_kpe, c, c_ctx, norm_g, w_ada, b_ada,
           e_w_in, e_conv_w, e_q_norm_g, e_w_qb, e_kv_norm_g, e_w_kvb, e_w_out,
           o_w_in, o_ln_g, o_ln_b, o_w_s, o_b_s, o_w_out, final_g):
    f32 = lambda a: np.asarray(a, dtype=np.float32)
    x_prompt, x_sample, cache_ckv, cache_kpe, c, c_ctx = map(f32, (x_prompt, x_sample, cache_ckv, cache_kpe, c, c_ctx))
    norm_g, w_ada, b_ada, final_g = map(f32, (norm_g, w_ada, b_ada, final_g))
    ncores = 8
    if "nc" not in _NC_CACHE:
        _NC_CACHE["nc"] = build_program()
    nc = _NC_CACHE["nc"]

    shared = {}
    shared["ngT"] = np.ascontiguousarray(np.concatenate([_pk(norm_g[l], 16) for l in range(4)] + [_pk(final_g, 16)], axis=1))
    shared["w_ada"] = np.ascontiguousarray(w_ada)
    shared["b_adaT"] = np.ascontiguousarray(np.concatenate([_pk(b_ada[l], 48) for l in range(4)], axis=1))
    ecols, ocols = _even_cols(), _odd_cols()
    shared["e_win"] = np.stack([_tile_w(f32(e_w_in[i]), ecols) for i in range(2)])
    shared["e_whd"] = np.stack([_head_w(f32(e_w_qb[i]), f32(e_w_kvb[i])) for i in range(2)])
    wo_cols = [pn * 256 + np.arange(256) for pn in range(8)]
    shared["e_wout"] = np.stack([_tile_w(f32(e_w_out[i]), wo_cols) for i in range(2)])
    es_ = []
    for i in range(2):
        cw = f32(e_conv_w[i])
        cw_p = cw.reshape(3, 16, 128).transpose(2, 1, 0).reshape(128, 48)
        es_.append(np.concatenate([cw_p, _pk(f32(e_q_norm_g[i]), 4), _pk(f32(e_kv_norm_g[i]), 4)], axis=1))
    shared["e_small"] = np.ascontiguousarray(np.concatenate(es_, axis=1))
    shared["o_win"] = np.stack([_tile_w(f32(o_w_in[i]), ocols) for i in range(2)])
    shared["o_wout"] = np.stack([_tile_w(f32(o_w_out[i]), [j * 128 + np.arange(128) for j in range(16)]) for i in range(2)])
    shared["o_wsT"] = np.ascontiguousarray(np.stack([f32(o_w_s[i]).transpose(2, 0, 1).reshape(128, 2048) for i in range(2)]))
    shared["o_small"] = np.ascontiguousarray(np.concatenate(
        [np.concatenate([_pk(f32(o_ln_g[i]), 32), _pk(f32(o_ln_b[i]), 32)], axis=1) for i in range(2)], axis=1))
    shared["o_bs"] = np.ascontiguousarray(f32(o_b_s).reshape(2, 1, 2048))
    shared["ropeT"] = _rope_tables()
    shared["ident"] = np.eye(128, dtype=np.float32)

    in_maps = []
    for i in range(ncores):
        m = dict(shared)
        m["x_tok"] = np.ascontiguousarray(np.concatenate([x_sample[i], x_prompt[2 * i], x_prompt[2 * i + 1]], axis=0))
        m["cckv"] = np.ascontiguousarray(cache_ckv[i])
        m["ckpe"] = np.ascontiguousarray(cache_kpe[i])
        cond = np.stack([c[i], c_ctx], axis=1)
        m["condT"] = np.ascontiguousarray(cond.reshape(16, 128, 2).transpose(1, 0, 2).reshape(128, 32))
        in_maps.append(m)
    res = run_bass_kernel_spmd(nc, in_maps, core_ids=list(range(ncores)))
    y_prompt = np.zeros((16, 256, D), np.float32)
    y_sample = np.zeros((8, NS, D), np.float32)
    new_ckv = np.zeros((16, 2, 256, 512), np.float32)
    new_kpe = np.zeros((16, 2, 256, 64), np.float32)
    for i in range(ncores):
        r = res.results[i]
        y_sample[i] = r["y_tok"][:NS]
        y_prompt[2 * i] = r["y_tok"][NS:NS + 256]
        y_prompt[2 * i + 1] = r["y_tok"][NS + 256:]
        new_ckv[2 * i:2 * i + 2] = r["new_ckv"]
        new_kpe[2 * i:2 * i + 2] = r["new_kpe"]
    return (y_prompt, y_sample, new_ckv, new_kpe)
```

```python
import numpy as np
from contextlib import ExitStack
import concourse.bass as bass
import concourse.mybir as mybir
from concourse.bass_utils import run_bass_kernel_spmd

F32 = mybir.dt.float32
BF16 = mybir.dt.bfloat16
AF = mybir.ActivationFunctionType
ALU = mybir.AluOpType

D = 2048
KC = 16
T = 4608
NS = 4096
NKEY = 5120
EPS = 1e-6
SCALE = 192 ** -0.5
E_NP = 23
O_NP = 24
DEPTH = 4


class Buf:
    __slots__ = ("name", "w", "r")

    def __init__(self, name=""):
        self.name = name
        self.w = {}
        self.r = {}


class FW:
    NDMA = 48

    def __init__(self, nc, es):
        self.nc = nc
        self.eng = {"pe": nc.tensor, "act": nc.scalar, "dve": nc.vector, "pool": nc.gpsimd, "sp": nc.sync}
        self.sem = {}
        self.cnt = {}
        for e in ("pe", "act", "dve", "pool"):
            self.sem[e] = es.enter_context(nc.semaphore("s_" + e))
            self.cnt[e] = 0
        self.dpool = {"sp": [], "pool": []}
        for q, n in (("sp", 32), ("pool", 24)):
            for i in range(n):
                k = ("d" + q, i)
                self.sem[k] = es.enter_context(nc.semaphore("s_d%s%d" % (q, i)))
                self.cnt[k] = 0
                self.dpool[q].append(k)
        self.dnext = {"sp": 0, "pool": 0}
        self.seen = {e: {} for e in self.eng}
        self.out_toks = []

    def _waits(self, eng, reads, writes, extra=()):
        need = {}

        def add(k, v):
            if need.get(k, 0) < v:
                need[k] = v
        for b in reads:
            for k, v in b.w.items():
                add(k, v)
        for b in writes:
            for k, v in b.w.items():
                add(k, v)
            for k, v in b.r.items():
                add(k, v)
        if eng == "pe":
            need.pop("pe", None)
        for k, v in extra:
            add(k, v)
        seen = self.seen[eng]
        e = self.eng[eng]
        for k, v in need.items():
            if seen.get(k, 0) < v:
                seen[k] = v
                e.wait_ge(self.sem[k], v)

    def _commit(self, tok, reads, writes):
        k, v = tok
        for b in writes:
            b.w[k] = v
        for b in reads:
            b.r[k] = v

    def op(self, eng, fn, reads=(), writes=()):
        self._waits(eng, reads, writes)
        ins = fn(self.eng[eng])
        self.cnt[eng] += 1
        ins.then_inc(self.sem[eng], 1)
        tok = (eng, self.cnt[eng])
        self._commit(tok, reads, writes)
        return tok

    def dma(self, q, out, in_, reads=(), writes=(), is_output=False):
        pl = self.dpool[q]
        k = pl[self.dnext[q] % len(pl)]
        self.dnext[q] += 1
        extra = [(k, self.cnt[k])] if self.cnt[k] else []
        self._waits(q, reads, writes, extra)
        ins = self.eng[q].dma_start(out=out, in_=in_)
        self.cnt[k] += 16
        ins.then_inc(self.sem[k], 16)
        tok = (k, self.cnt[k])
        self._commit(tok, reads, writes)
        if is_output:
            self.out_toks.append(tok)
        return tok

    def barrier(self):
        toks = [(k, v) for k, v in self.cnt.items() if v > 0]
        for eng in self.eng:
            seen = self.seen[eng]
            e = self.eng[eng]
            for k, v in toks:
                if k == eng:
                    continue
                if seen.get(k, 0) < v:
                    seen[k] = v
                    e.wait_ge(self.sem[k], v)


def sample_tiles():
    tiles = []
    step = 456
    lo = 0
    while lo < NS:
        hi = min(lo + step, NS)
        c0 = max(lo - 1, 0)
        c1 = min(hi + 1, NS)
        tiles.append(dict(c0=c0, n=c1 - c0, outs=[(lo - c0, hi - c0, 0, c1 - c0, lo == 0, hi == NS)],
                          cond=0, rope=True))
        lo = hi
    return tiles


def build_program(depth=DEPTH, stop=None):
    nc = bass.Bass("TRN2", target_bir_lowering=False)
    top = ExitStack()
    fw = FW(nc, top)

    def din(name, shape, dt=F32):
        return nc.dram_tensor(name, list(shape), dt, kind="ExternalInput").ap()

    def dout(name, shape, dt=F32):
        return nc.dram_tensor(name, list(shape), dt, kind="ExternalOutput").ap()

    def dint(name, shape, dt):
        return nc.dram_tensor(name, list(shape), dt).ap()

    x_tok = din("x_tok", [T, D])
    cckv = din("cckv", [2, 512, 512])
    ckpe = din("ckpe", [2, 512, 64])
    condT = din("condT", [128, KC * 2])
    ngT = din("ngT", [128, 5 * KC])
    w_ada = din("w_ada", [DEPTH, D, 3 * D])
    b_adaT = din("b_adaT", [128, DEPTH * 48])
    e_win = din("e_win", [2, E_NP * 128, KC * 512])
    e_whd = din("e_whd", [2, 16 * 128, 4 * 512])
    e_wout = din("e_wout", [2, 8 * 128, 32 * 256])
    e_small = din("e_small", [128, 2 * (48 + 4 + 4)])
    o_win = din("o_win", [2, O_NP * 128, KC * 512])
    o_wout = din("o_wout", [2, 16 * 128, 32 * 128])
    o_wsT = din("o_wsT", [2, 128, 16 * 128])
    o_small = din("o_small", [128, 2 * 64])
    o_bs = din("o_bs", [2, 1, 16 * 128])
    ropeT = din("ropeT", [2, 64, NS])
    ident_d = din("ident", [128, 128])

    y_tok = dout("y_tok", [T, D])
    new_ckv = dout("new_ckv", [2, 2, 256, 512])
    new_kpe = dout("new_kpe", [2, 2, 256, 64])

    xT = dint("xT", [D, T], F32)
    qnT = dint("qnT", [512, T], BF16)
    smgT = dint("smgT", [D, T], BF16)
    ymixT = dint("ymixT", [2 * D, T], BF16)
    e_win_b = dint("e_win_b", [2, E_NP * 128, KC * 512], BF16)
    e_whd_b = dint("e_whd_b", [2, 16 * 128, 4 * 512], BF16)
    e_wout_b = dint("e_wout_b", [2, 8 * 128, 32 * 256], BF16)
    o_win_b = dint("o_win_b", [2, O_NP * 128, KC * 512], BF16)
    o_wout_b = dint("o_wout_b", [2, 16 * 128, 32 * 128], BF16)
    B_xT = Buf("xT")
    B_qnT = Buf("qnT")
    B_smgT = Buf("smgT")
    B_ymix = Buf("ymix")
    B_wb = {n: [Buf(n + "0"), Buf(n + "1")] for n in ("e_win", "e_whd", "e_wout", "o_win", "o_wout")}

    uid = [0]

    def S(es, name, shape, dt=F32):
        uid[0] += 1
        return es.enter_context(nc.sbuf_tensor("%s_%d" % (name, uid[0]), list(shape), dt))

    ident = S(top, "ident", [128, 128])
    ones_f = S(top, "ones_f", [128, 128])
    ones_b = S(top, "ones_b", [128, 128], BF16)
    epsb = S(top, "epsb", [128, 1])
    Gt = S(top, "Gt", [128, DEPTH * KC * 2])
    SHt = S(top, "SHt", [128, DEPTH * KC * 2])
    GAt = S(top, "GAt", [128, DEPTH * KC * 2])
    ngs = S(top, "ngs", [128, 5 * KC])
    esm = S(top, "esm", [128, 2 * 56])
    osm = S(top, "osm", [128, 2 * 64])
    B_const = Buf("const")
    B_mod = Buf("mod")
    pbig = [top.enter_context(nc.psum_tensor("pb%d" % i, [128, 1024], F32)) for i in range(4)]
    psum = [pbig[i // 2][:, (i % 2) * 512:(i % 2) * 512 + 512] for i in range(8)]
    B_ps = [Buf("ps%d" % i) for i in range(8)]

    def mod_ap(t, l, f, cond):
        o = (l * KC + f) * 2 + cond
        return t[:, o:o + 1]

    fw.op("pool", lambda e: e.memset(ones_f[:], 1.0), writes=[B_const])
    fw.op("pool", lambda e: e.memset(ones_b[:], 1.0), writes=[B_const])
    fw.op("pool", lambda e: e.memset(epsb[:], EPS), writes=[B_const])
    fw.dma("sp", ident[:], ident_d, writes=[B_const])
    fw.dma("sp", ngs[:], ngT, writes=[B_const])
    fw.dma("sp", esm[:], e_small, writes=[B_const])
    fw.dma("sp", osm[:], o_small, writes=[B_const])

    pieces = []

    def add_precast(name, src, dst, i, lyr):
        rows = src.shape[1]
        bufs = []
        r = 0
        while r < rows:
            rr = min(256, rows - r)
            b = Buf()
            bufs.append(b)
            pieces.append((lyr, src[i, r:r + rr, :], dst[i, r:r + rr, :], b))
            r += rr
        B_wb[name][i] = bufs

    def bg_pump(n=1, upto=None):
        while pieces and (n > 0 or (upto is not None and pieces[0][0] <= upto)):
            lyr, src, dst, b = pieces.pop(0)
            fw.dma("pool", dst, src, writes=[b])
            n -= 1

    def wbuf(name, i, row0):
        return B_wb[name][i][row0 // 256]
    for l_ in range(depth):
        i_ = l_ // 2
        if l_ % 2 == 0:
            add_precast("e_win", e_win, e_win_b, i_, l_)
            add_precast("e_whd", e_whd, e_whd_b, i_, l_)
            add_precast("e_wout", e_wout, e_wout_b, i_, l_)
        else:
            add_precast("o_win", o_win, o_win_b, i_, l_)
            add_precast("o_wout", o_wout, o_wout_b, i_, l_)

    def pro_x():
        with ExitStack() as es:
            xin = [S(es, "xin%d" % i, [128, D]) for i in range(2)]
            B_xin = [Buf(), Buf()]
            xst = [S(es, "xst%d" % i, [128, KC * 128]) for i in range(2)]
            B_xst = [Buf(), Buf()]
            xT_v = xT.rearrange("(f p) t -> p f t", p=128)
            for st in range(T // 128):
                b = st % 2
                fw.dma("sp", xin[b][:], x_tok[st * 128:(st + 1) * 128, :], writes=[B_xin[b]])
                for g in range(4):
                    bk = (st * 4 + g) % 8

                    def tr(e, b=b, g=g, bk=bk):
                        for j in range(4):
                            f = g * 4 + j
                            ins = e.transpose(psum[bk][:, j * 128:(j + 1) * 128], xin[b][:, f * 128:(f + 1) * 128], ident[:])
                        return ins
                    fw.op("pe", tr, reads=[B_xin[b], B_const], writes=[B_ps[bk]])
                    eng = "act" if g % 2 == 0 else "dve"
                    if eng == "act":
                        fw.op("act", lambda e, b=b, g=g, bk=bk: e.activation(out=xst[b][:, g * 512:(g + 1) * 512], in_=psum[bk][:, 0:512], func=AF.Copy),
                              reads=[B_ps[bk]], writes=[B_xst[b]])
                    else:
                        fw.op("dve", lambda e, b=b, g=g, bk=bk: e.tensor_copy(out=xst[b][:, g * 512:(g + 1) * 512], in_=psum[bk][:, 0:512]),
                              reads=[B_ps[bk]], writes=[B_xst[b]])
                fw.dma("pool", xT_v[:, :, st * 128:(st + 1) * 128], xst[b][:].rearrange("p (f t) -> p f t", f=KC),
                       reads=[B_xst[b]], writes=[B_xT])

    def pro_ada():
        with ExitStack() as es:
            wa = [S(es, "wa%d" % i, [128, 3 * D]) for i in range(2)]
            B_wa = [Buf(), Buf()]
            sc = S(es, "sc", [128, KC * 2])
            macc = S(es, "macc", [128, 96])
            bad = S(es, "bad", [128, DEPTH * 48])
            B_sc = Buf()
            B_macc = Buf()
            fw.dma("sp", sc[:], condT, writes=[B_sc])
            fw.dma("sp", bad[:], b_adaT, writes=[B_sc])
            fw.op("act", lambda e: e.activation(out=sc[:], in_=sc[:], func=AF.Silu), reads=[B_sc], writes=[B_sc])
            step = 0
            for l in range(depth):
                for k in range(KC):
                    b = step % 2
                    bk = step % 2
                    step += 1
                    fw.dma("sp", wa[b][:], w_ada[l, k * 128:(k + 1) * 128, :], writes=[B_wa[b]])

                    def mm(e, b=b, k=k, bk=bk):
                        for j in range(48):
                            ins = e.matmul(psum[bk][:, 2 * j:2 * j + 2], wa[b][:, j * 128:(j + 1) * 128], sc[:, 2 * k:2 * k + 2],
                                           start=True, stop=True)
                        return ins
                    fw.op("pe", mm, reads=[B_wa[b], B_sc], writes=[B_ps[bk]])
                    if k == 0:
                        fw.op("dve", lambda e, bk=bk: e.tensor_copy(out=macc[:], in_=psum[bk][:, 0:96]), reads=[B_ps[bk]], writes=[B_macc])
                    else:
                        fw.op("dve", lambda e, bk=bk: e.tensor_tensor(out=macc[:], in0=macc[:], in1=psum[bk][:, 0:96], op=ALU.add),
                              reads=[B_ps[bk], B_macc], writes=[B_macc])
                m3 = macc[:].rearrange("p (j c) -> p j c", c=2)
                for c in range(2):
                    fw.op("dve", lambda e, c=c, l=l: e.tensor_tensor(out=m3[:, :, c], in0=m3[:, :, c], in1=bad[:, l * 48:(l + 1) * 48], op=ALU.add),
                          reads=[B_macc, B_sc], writes=[B_macc])
                lo = l * KC * 2
                G3 = Gt[:, lo:lo + 32].rearrange("p (f c) -> p f c", c=2)
                for c in range(2):
                    fw.op("dve", lambda e, c=c, l=l, G3=G3: e.scalar_tensor_tensor(out=G3[:, :, c], in0=m3[:, 16:32, c], scalar=1.0,
                                                                                in1=ngs[:, l * KC:(l + 1) * KC], op0=ALU.add, op1=ALU.mult),
                          reads=[B_macc, B_const], writes=[B_mod])
                fw.op("dve", lambda e, lo=lo: e.tensor_copy(out=SHt[:, lo:lo + 32], in_=macc[:, 0:32]), reads=[B_macc], writes=[B_mod])
                fw.op("dve", lambda e, lo=lo: e.tensor_copy(out=GAt[:, lo:lo + 32], in_=macc[:, 64:96]), reads=[B_macc], writes=[B_mod])
    if stop != 'precast':
        pro_x()
        fw.barrier()
    bg_pump(0, upto=0)
    if stop not in ('precast', 'xT'):
        pro_ada()
    fw.barrier()

    def make_h(l, c0, n, cond, hT, B_h, xc, B_xc, sq, B_sq, rstd, B_rstd, ss_bk, ctr):
        for f in range(KC):
            r = ctr[0] % len(xc)
            ctr[0] += 1
            fw.dma("sp", xc[r][:, 0:n], xT[f * 128:(f + 1) * 128, c0:c0 + n], reads=[B_xT], writes=[B_xc[r]])
            q = f % 2
            fw.op("act", lambda e, r=r, q=q: e.activation(out=sq[q][:, 0:n], in_=xc[r][:, 0:n], func=AF.Square),
                  reads=[B_xc[r]], writes=[B_sq[q]])
            fw.op("pe", lambda e, q=q, f=f: e.matmul(psum[ss_bk][:, 0:n], ones_f[:], sq[q][:, 0:n], start=(f == 0), stop=(f == KC - 1)),
                  reads=[B_sq[q], B_const], writes=[B_ps[ss_bk]])
        fw.op("act", lambda e: e.activation(out=rstd[:, 0:n], in_=psum[ss_bk][:, 0:n], func=AF.Sqrt, bias=epsb[:, 0:1], scale=1.0 / D),
              reads=[B_ps[ss_bk], B_const], writes=[B_rstd])
        fw.op("dve", lambda e: e.reciprocal(out=rstd[:, 0:n], in_=rstd[:, 0:n]), reads=[B_rstd], writes=[B_rstd])
        for f in range(KC):
            r = ctr[0] % len(xc)
            ctr[0] += 1
            fw.dma("sp", xc[r][:, 0:n], xT[f * 128:(f + 1) * 128, c0:c0 + n], reads=[B_xT], writes=[B_xc[r]])
            fw.op("dve", lambda e, r=r: e.tensor_tensor(out=xc[r][:, 0:n], in0=xc[r][:, 0:n], in1=rstd[:, 0:n], op=ALU.mult),
                  reads=[B_xc[r], B_rstd], writes=[B_xc[r]])
            fw.op("act", lambda e, r=r, f=f: e.activation(out=hT[:, f * 512:f * 512 + n], in_=xc[r][:, 0:n], func=AF.Identity,
                                                        bias=mod_ap(SHt, l, f, cond), scale=mod_ap(Gt, l, f, cond)),
                  reads=[B_xc[r], B_mod], writes=[B_h])

    def group_mm(bk, lhs_fn, rhs_fn, nk, n, reads, m=128):
        def mm(e):
            for k in range(nk):
                ins = e.matmul(psum[bk][0:m, 0:n], lhs_fn(k), rhs_fn(k), start=(k == 0), stop=(k == nk - 1))
            return ins
        fw.op("pe", mm, reads=reads, writes=[B_ps[bk]])

    def evac(eng, out, in_, reads, writes):
        if eng == "act":
            fw.op("act", lambda e: e.activation(out=out, in_=in_, func=AF.Copy), reads=reads, writes=writes)
        else:
            fw.op("dve", lambda e: e.tensor_copy(out=out, in_=in_), reads=reads, writes=writes)

    def out_transposed(src_fn, nfeat_chunks, fw_feat, ntok, dst_fn, stage, B_stage, bkc, reads):
        for ts in range(ntok // 128):
            b = ts % 2
            for c in range(nfeat_chunks):
                bk = bkc[0] % 8
                bkc[0] += 1
                fw.op("pe", lambda e, c=c, ts=ts, bk=bk: e.transpose(psum[bk][:, 0:fw_feat], src_fn(c)[:, ts * 128:(ts + 1) * 128], ident[0:fw_feat, 0:fw_feat]),
                      reads=reads + [B_const], writes=[B_ps[bk]])
                evac("dve" if c % 2 else "act", stage[b][:, c * fw_feat:(c + 1) * fw_feat], psum[bk][:, 0:fw_feat], [B_ps[bk]], [B_stage[b]])
            fw.dma("pool", dst_fn(ts), stage[b][:, 0:nfeat_chunks * fw_feat], reads=[B_stage[b]], is_output=True)

    def even_layer(l):
        i = l // 2
        bg_pump(0, upto=l)
        with ExitStack() as esL:
            ckvT = S(esL, "ckvT", [128, 4 * NKEY], BF16)
            kpeT = S(esL, "kpeT", [128, NKEY], BF16)
            B_ckvT = Buf("ckvT")
            B_kpeT = Buf("kpeT")
            fw.op("pool", lambda e: e.memset(kpeT[64:128, :], 0.0), writes=[B_kpeT])
            eo = i * 56
            convw = lambda f, tap: esm[:, eo + f * 3 + tap: eo + f * 3 + tap + 1]
            qg = lambda c: esm[:, eo + 48 + c: eo + 48 + c + 1]
            kg = lambda c: esm[:, eo + 52 + c: eo + 52 + c + 1]

            with ExitStack() as es:
                cin = [S(es, "cin%d" % j, [128, 576]) for j in range(2)]
                B_cin = [Buf(), Buf()]
                for kt in range(4):
                    b = kt % 2
                    fw.dma("sp", cin[b][:, 0:512], cckv[i, kt * 128:(kt + 1) * 128, :], writes=[B_cin[b]])
                    fw.dma("sp", cin[b][:, 512:576], ckpe[i, kt * 128:(kt + 1) * 128, :], writes=[B_cin[b]])
                    for c in range(4):
                        bk = (kt * 5 + c) % 8
                        fw.op("pe", lambda e, b=b, c=c, bk=bk: e.transpose(psum[bk][:, 0:128], cin[b][:, c * 128:(c + 1) * 128], ident[:]),
                              reads=[B_cin[b], B_const], writes=[B_ps[bk]])
                        evac("act" if c % 2 else "dve", ckvT[:, c * NKEY + kt * 128: c * NKEY + (kt + 1) * 128], psum[bk][:, 0:128], [B_ps[bk]], [B_ckvT])
                    bk = (kt * 5 + 4) % 8
                    fw.op("pe", lambda e, b=b, bk=bk: e.transpose(psum[bk][0:64, 0:128], cin[b][:, 512:576], ident[:]),
                          reads=[B_cin[b], B_const], writes=[B_ps[bk]])
                    evac("dve", kpeT[0:64, kt * 128:(kt + 1) * 128], psum[bk][0:64, 0:128], [B_ps[bk]], [B_kpeT])
            fw.barrier()

            with ExitStack() as es:
                xc = [S(es, "xc%d" % j, [128, 512]) for j in range(3)]
                B_xc = [Buf() for _ in xc]
                sq = [S(es, "sq%d" % j, [128, 512]) for j in range(2)]
                B_sq = [Buf(), Buf()]
                rstd = S(es, "rstd", [128, 512])
                B_rstd = Buf()
                hT = [S(es, "hT%d" % j, [128, KC * 512], BF16) for j in range(2)]
                B_h = [Buf(), Buf()]
                wp = [S(es, "wp%d" % j, [128, KC * 512], BF16) for j in range(3)]
                B_wp = [Buf(), Buf(), Buf()]
                ccs = S(es, "ccs", [128, 512]); pt = S(es, "pt", [128, 512]); sg = S(es, "sg", [128, 512])
                cbg = S(es, "cbg", [128, 512]); acc = S(es, "acc", [128, 512])
                B_ccs, B_pt, B_sg, B_cbg, B_acc = Buf(), Buf(), Buf(), Buf(), Buf()
                yo = [S(es, "yo%d" % j, [128, 512], BF16) for j in range(2)]
                B_yo = [Buf(), Buf()]
                raw = S(es, "raw", [128, 4 * 512]); B_raw = Buf()
                sqq = [S(es, "sqq%d" % j, [128, 512]) for j in range(2)]; B_sqq = [Buf(), Buf()]
                rq = S(es, "rq", [128, 512]); B_rq = Buf()
                nrm = S(es, "nrm", [128, 4 * 512]); B_nrm = Buf()
                qno = S(es, "qno", [128, 4 * 512], BF16); B_qno = Buf()
                rt = S(es, "rt", [64, 2 * 512]); B_rt = Buf()
                kr = S(es, "kr", [64, 2 * 512]); B_kr = Buf()
                stage = [S(es, "stg%d" % j, [128, 512]) for j in range(2)]; B_stage = [Buf(), Buf()]
                ctr = [0]
                bkc = [0]
                pc = [0]

                tiles = sample_tiles()
                tiles.append(dict(c0=NS, n=512, outs=[(0, 256, 0, 256, True, True), (256, 512, 256, 512, True, True)], cond=1, rope=False))
                wv = e_win_b[i].rearrange("(q p) c -> q p c", p=128)

                def ensure_panels(upto):
                    while pc[0] <= upto and pc[0] < len(tiles) * E_NP:
                        pn_ = pc[0] % E_NP
                        b_ = pc[0] % 3
                        pc[0] += 1
                        fw.dma("sp", wp[b_][:], wv[pn_], reads=[wbuf("e_win", i, pn_ * 128)], writes=[B_wp[b_]])

                def nextbank():
                    bk = 1 + (bkc[0] % 6)
                    bkc[0] += 1
                    return bk

                make_h(l, tiles[0]["c0"], tiles[0]["n"], tiles[0]["cond"], hT[0], B_h[0], xc, B_xc, sq, B_sq, rstd, B_rstd, 0, ctr)
                for ti, tl in enumerate(tiles):
                    c0, n, cond = tl["c0"], tl["n"], tl["cond"]
                    hb = ti % 2
                    h = hT[hb]
                    if tl["rope"]:
                        fw.dma("sp", rt[:, 0:n], ropeT[0, :, c0:c0 + n], writes=[B_rt])
                        fw.dma("sp", rt[:, 512:512 + n], ropeT[1, :, c0:c0 + n], writes=[B_rt])
                    for pn in range(E_NP):
                        sidx = ti * E_NP + pn
                        ensure_panels(sidx + 2)
                        cur = sidx % 3
                        if pn == 10 and ti + 1 < len(tiles):
                            tn = tiles[ti + 1]
                            make_h(l, tn["c0"], tn["n"], tn["cond"], hT[1 - hb], B_h[1 - hb], xc, B_xc, sq, B_sq, rstd, B_rstd, 0, ctr)
                        w = wp[cur]
                        rd = [B_wp[cur], B_h[hb]]

                        def lhs(g, m0=0, m1=128):
                            return lambda k: w[:, k * 512 + g * 128 + m0: k * 512 + g * 128 + m1]
                        rhs = lambda k: h[:, k * 512:k * 512 + n]
                        if pn < 16:
                            f = pn
                            bcb, bcc, bcx, bcg = nextbank(), nextbank(), nextbank(), nextbank()
                            group_mm(bcc, lhs(1), rhs, KC, n, rd)
                            group_mm(bcx, lhs(2), rhs, KC, n, rd)
                            group_mm(bcg, lhs(3), rhs, KC, n, rd)
                            group_mm(bcb, lhs(0), rhs, KC, n, rd)
                            fw.op("act", lambda e, bcc=bcc: e.activation(out=ccs[:, 0:n], in_=psum[bcc][:, 0:n], func=AF.Copy),
                                  reads=[B_ps[bcc]], writes=[B_ccs])
                            fw.op("dve", lambda e, bcx=bcx: e.tensor_tensor(out=pt[:, 0:n], in0=psum[bcx][:, 0:n], in1=ccs[:, 0:n], op=ALU.mult),
                                  reads=[B_ps[bcx], B_ccs], writes=[B_pt])
                            fw.op("act", lambda e, bcg=bcg: e.activation(out=sg[:, 0:n], in_=psum[bcg][:, 0:n], func=AF.Silu),
                                  reads=[B_ps[bcg]], writes=[B_sg])
                            fw.op("dve", lambda e, bcb=bcb: e.tensor_tensor(out=cbg[:, 0:n], in0=psum[bcb][:, 0:n], in1=sg[:, 0:n], op=ALU.mult),
                                  reads=[B_ps[bcb], B_sg], writes=[B_cbg])
                            yb = f % 2
                            for (oa, ob, sa, sb, zl, zr) in tl["outs"]:
                                fw.op("act", lambda e, oa=oa, ob=ob, f=f: e.activation(out=acc[:, oa:ob], in_=pt[:, oa:ob], func=AF.Identity, scale=convw(f, 1)),
                                      reads=[B_pt, B_const], writes=[B_acc])
                                la = max(oa, sa + 1)
                                fw.op("dve", lambda e, la=la, ob=ob, f=f: e.scalar_tensor_tensor(out=acc[:, la:ob], in0=pt[:, la - 1:ob - 1], scalar=convw(f, 0),
                                                                                             in1=acc[:, la:ob], op0=ALU.mult, op1=ALU.add),
                                      reads=[B_pt, B_acc, B_const], writes=[B_acc])
                                rb = min(ob, sb - 1)
                                fw.op("dve", lambda e, oa=oa, rb=rb, f=f: e.scalar_tensor_tensor(out=acc[:, oa:rb], in0=pt[:, oa + 1:rb + 1], scalar=convw(f, 2),
                                                                                             in1=acc[:, oa:rb], op0=ALU.mult, op1=ALU.add),
                                      reads=[B_pt, B_acc, B_const], writes=[B_acc])
                                fw.op("dve", lambda e, oa=oa, ob=ob, yb=yb: e.tensor_tensor(out=yo[yb][:, oa:ob], in0=acc[:, oa:ob], in1=cbg[:, oa:ob], op=ALU.mult),
                                      reads=[B_acc, B_cbg], writes=[B_yo[yb]])
                                fw.dma("pool", ymixT[f * 128:(f + 1) * 128, c0 + oa:c0 + ob], yo[yb][:, oa:ob], reads=[B_yo[yb]], writes=[B_ymix])
                        elif pn in (16, 17):
                            ssb = 7
                            for c in range(4):
                                bk = nextbank()
                                group_mm(bk, lhs(c), rhs, KC, n, rd)
                                fw.op("act", lambda e, bk=bk, c=c: e.activation(out=raw[:, c * 512:c * 512 + n], in_=psum[bk][:, 0:n], func=AF.Copy),
                                      reads=[B_ps[bk]], writes=[B_raw])
                                q = c % 2
                                fw.op("act", lambda e, bk=bk, q=q: e.activation(out=sqq[q][:, 0:n], in_=psum[bk][:, 0:n], func=AF.Square),
                                      reads=[B_ps[bk]], writes=[B_sqq[q]])
                                fw.op("pe", lambda e, q=q, c=c: e.matmul(psum[ssb][:, 0:n], ones_f[:], sqq[q][:, 0:n], start=(c == 0), stop=(c == 3)),
                                      reads=[B_sqq[q], B_const], writes=[B_ps[ssb]])
                            fw.op("act", lambda e: e.activation(out=rq[:, 0:n], in_=psum[ssb][:, 0:n], func=AF.Sqrt, bias=epsb[:, 0:1], scale=1.0 / 512),
                                  reads=[B_ps[ssb], B_const], writes=[B_rq])
                            fw.op("dve", lambda e: e.reciprocal(out=rq[:, 0:n], in_=rq[:, 0:n]), reads=[B_rq], writes=[B_rq])
                            for c in range(4):
                                fw.op("dve", lambda e, c=c: e.tensor_tensor(out=raw[:, c * 512:c * 512 + n], in0=raw[:, c * 512:c * 512 + n], in1=rq[:, 0:n], op=ALU.mult),
                                      reads=[B_raw, B_rq], writes=[B_raw])
                                for (oa, ob, sa, sb, zl, zr) in tl["outs"]:
                                    if pn == 16:
                                        fw.op("act", lambda e, c=c, oa=oa, ob=ob: e.activation(out=qno[:, c * 512 + oa:c * 512 + ob], in_=raw[:, c * 512 + oa:c * 512 + ob],
                                                                                                func=AF.Identity, scale=qg(c)),
                                              reads=[B_raw, B_const], writes=[B_qno])
                                    else:
                                        kc0 = c * NKEY + 512 + c0
                                        fw.op("act", lambda e, c=c, oa=oa, ob=ob, kc0=kc0: e.activation(out=ckvT[:, kc0 + oa:kc0 + ob], in_=raw[:, c * 512 + oa:c * 512 + ob],
                                                                                                         func=AF.Identity, scale=kg(c)),
                                              reads=[B_raw, B_const], writes=[B_ckvT])
                                if pn == 17 and cond == 1:
                                    fw.op("act", lambda e, c=c: e.activation(out=nrm[:, c * 512:c * 512 + n], in_=raw[:, c * 512:c * 512 + n], func=AF.Identity, scale=kg(c)),
                                          reads=[B_raw, B_const], writes=[B_nrm])
                            if pn == 16:
                                for (oa, ob, sa, sb, zl, zr) in tl["outs"]:
                                    fw.dma("pool", qnT.rearrange("(c p) t -> p c t", p=128)[:, :, c0 + oa:c0 + ob],
                                           qno[:].rearrange("p (c t) -> p c t", c=4)[:, :, oa:ob], reads=[B_qno], writes=[B_qnT])
                            elif cond == 1:
                                for pr in range(2):
                                    out_transposed(lambda c, pr=pr: nrm[:, c * 512 + pr * 256:c * 512 + pr * 256 + 256], 4, 128, 256,
                                                   lambda ts, pr=pr: new_ckv[pr, i, ts * 128:(ts + 1) * 128, :], stage, B_stage, bkc, [B_nrm])
                        elif pn < 22:
                            for g in range(4):
                                f = (pn - 18) * 4 + g
                                bk = nextbank()
                                group_mm(bk, lhs(g), rhs, KC, n, rd)
                                yb = g % 2
                                fw.op("act", lambda e, bk=bk, yb=yb: e.activation(out=yo[yb][:, 0:n], in_=psum[bk][:, 0:n], func=AF.Silu),
                                      reads=[B_ps[bk]], writes=[B_yo[yb]])
                                for (oa, ob, sa, sb, zl, zr) in tl["outs"]:
                                    fw.dma("pool", smgT[f * 128:(f + 1) * 128, c0 + oa:c0 + ob], yo[yb][:, oa:ob], reads=[B_yo[yb]], writes=[B_smgT])
                        else:
                            bka, bkb = nextbank(), nextbank()
                            group_mm(bka, lhs(0, 0, 64), rhs, KC, n, rd, m=64)
                            kc0 = 512 + c0
                            if tl["rope"]:
                                group_mm(bkb, lhs(0, 64, 128), rhs, KC, n, rd, m=64)
                                fw.op("dve", lambda e, bka=bka: e.tensor_tensor(out=kr[:, 0:n], in0=psum[bka][0:64, 0:n], in1=rt[:, 0:n], op=ALU.mult),
                                      reads=[B_ps[bka], B_rt], writes=[B_kr])
                                fw.op("dve", lambda e, bkb=bkb: e.tensor_tensor(out=kr[:, 512:512 + n], in0=psum[bkb][0:64, 0:n], in1=rt[:, 512:512 + n], op=ALU.mult),
                                      reads=[B_ps[bkb], B_rt], writes=[B_kr])
                                for (oa, ob, sa, sb, zl, zr) in tl["outs"]:
                                    fw.op("dve", lambda e, oa=oa, ob=ob, kc0=kc0: e.tensor_tensor(out=kpeT[0:64, kc0 + oa:kc0 + ob], in0=kr[:, oa:ob], in1=kr[:, 512 + oa:512 + ob], op=ALU.add),
                                          reads=[B_kr], writes=[B_kpeT])
                            else:
                                fw.op("act", lambda e, bka=bka: e.activation(out=kr[:, 0:n], in_=psum[bka][0:64, 0:n], func=AF.Copy),
                                      reads=[B_ps[bka]], writes=[B_kr])
                                fw.op("dve", lambda e, kc0=kc0: e.tensor_copy(out=kpeT[0:64, kc0:kc0 + n], in_=kr[:, 0:n]), reads=[B_kr], writes=[B_kpeT])
                                for pr in range(2):
                                    out_transposed(lambda c, pr=pr: kr[:, pr * 256:pr * 256 + 256], 1, 64, 256,
                                                   lambda ts, pr=pr: new_kpe[pr, i, ts * 128:(ts + 1) * 128, :], stage, B_stage, bkc, [B_kr])
            fw.barrier()

            with ExitStack() as es:
                whd = [S(es, "whd%d" % j, [128, 4 * 512], BF16) for j in range(2)]; B_whd = [Buf(), Buf()]
                KnT = [S(es, "KnT%d" % j, [128, NKEY], BF16) for j in range(2)]; B_Kn = [Buf(), Buf()]
                Vh = [S(es, "Vh%d" % j, [128, 40 * 128], BF16) for j in range(2)]; B_V = [Buf(), Buf()]
                qn = [S(es, "qn%d" % j, [128, 4 * 512], BF16) for j in range(2)]; B_qn = [Buf(), Buf()]
                Qn = [S(es, "Qn%d" % j, [128, 512], BF16) for j in range(2)]; B_Qn = [Buf(), Buf()]
                Qr = [S(es, "Qr%d" % j, [128, 512], BF16) for j in range(2)]; B_Qr = [Buf(), Buf()]
                for j in range(2):
                    fw.op("pool", lambda e: e.memset(Qr[j][64:128, :], 0.0), writes=[B_Qr[j]])
                rt = [S(es, "rt%d" % j, [64, 2 * 512]) for j in range(2)]; B_rt = [Buf(), Buf()]
                t1 = S(es, "t1", [64, 512]); t2 = S(es, "t2", [64, 512]); B_t1 = Buf(); B_t2 = Buf()
                PT = [S(es, "PT%d" % j, [128, 1024], BF16) for j in range(4)]; B_PT = [Buf() for _ in range(4)]
                rl = S(es, "rl", [128, 512]); B_rl = Buf()
                yf = S(es, "yf", [128, 512]); B_yf = Buf()
                smg = [S(es, "smg%d" % j, [128, 512], BF16) for j in range(2)]; B_smg = [Buf(), Buf()]
                yo = [S(es, "yao%d" % j, [128, 512], BF16) for j in range(2)]; B_yo = [Buf(), Buf()]
                mc = [0]
                def misc():
                    bk = 7
                    mc[0] += 1
                    return bk
                whv = e_whd_b[i].rearrange("(h p) c -> h p c", p=128)
                seqs = [dict(t0=0, n=NS, k0=0, k1=4608, rope=True),
                        dict(t0=NS, n=256, k0=4608, k1=4864, rope=False),
                        dict(t0=NS + 256, n=256, k0=4864, k1=5120, rope=False)]
                qtc = [0]
                ptc = [0]
                sc_ = [0]
                accA = S(es, "accA", [128, 1024]); accB = S(es, "accB", [128, 1024]); B_accA = Buf(); B_accB = Buf()
                items = []
                for sq_ in seqs:
                    QN_ = min(512, sq_["n"])
                    for qt in range(sq_["n"] // QN_):
                        items.append((sq_, qt, QN_))
                fw.dma("sp", whd[0][:], whv[0], reads=[wbuf("e_whd", i, 0)], writes=[B_whd[0]])

                def kv_proj(hd):
                    hb = hd % 2
                    w = whd[hb]
                    for ct in range(NKEY // 512):
                        bk = misc()
                        group_mm(bk, lambda k: w[:, k * 512 + 256:k * 512 + 384], lambda k: ckvT[:, k * NKEY + ct * 512:k * NKEY + (ct + 1) * 512],
                                 4, 512, [B_whd[hb], B_ckvT])
                        evac("act" if ct % 2 else "dve", KnT[hb][:, ct * 512:(ct + 1) * 512], psum[bk][:, 0:512], [B_ps[bk]], [B_Kn[hb]])
                    for g4 in range(10):
                        bk = misc()

                        def vmm(e):
                            for j in range(4):
                                kt = g4 * 4 + j
                                for k in range(4):
                                    ins = e.matmul(psum[bk][:, j * 128:(j + 1) * 128], ckvT[:, k * NKEY + kt * 128:k * NKEY + (kt + 1) * 128],
                                                   w[:, k * 512 + 384:k * 512 + 512], start=(k == 0), stop=(k == 3))
                            return ins
                        fw.op("pe", vmm, reads=[B_whd[hb], B_ckvT], writes=[B_ps[bk]])
                        evac("dve" if g4 % 2 else "act", Vh[hb][:, g4 * 512:(g4 + 1) * 512], psum[bk][:, 0:512], [B_ps[bk]], [B_V[hb]])

                def q_proj(hd, item, qb):
                    sq_, qt, QN = item
                    hb = hd % 2
                    w = whd[hb]
                    tq = sq_["t0"] + qt * QN
                    fw.dma("sp", qn[qb][:].rearrange("p (c t) -> p c t", c=4)[:, :, 0:QN],
                           qnT.rearrange("(c p) t -> p c t", p=128)[:, :, tq:tq + QN], reads=[B_qnT], writes=[B_qn[qb]])
                    fw.dma("sp", smg[qb][:, 0:QN], smgT[hd * 128:(hd + 1) * 128, tq:tq + QN], reads=[B_smgT], writes=[B_smg[qb]])
                    if sq_["rope"]:
                        fw.dma("sp", rt[qb][:, 0:QN], ropeT[0, :, tq:tq + QN], writes=[B_rt[qb]])
                        fw.dma("sp", rt[qb][:, 512:512 + QN], ropeT[1, :, tq:tq + QN], writes=[B_rt[qb]])
                    rhsq = lambda k: qn[qb][:, k * 512:k * 512 + QN]
                    bk = misc()
                    group_mm(bk, lambda k: w[:, k * 512:k * 512 + 128], rhsq, 4, QN, [B_whd[hb], B_qn[qb]])
                    evac("act", Qn[qb][:, 0:QN], psum[bk][:, 0:QN], [B_ps[bk]], [B_Qn[qb]])
                    bka = misc()
                    group_mm(bka, lambda k: w[:, k * 512 + 128:k * 512 + 192], rhsq, 4, QN, [B_whd[hb], B_qn[qb]], m=64)
                    if sq_["rope"]:
                        fw.op("dve", lambda e: e.tensor_tensor(out=t1[:, 0:QN], in0=psum[bka][0:64, 0:QN], in1=rt[qb][:, 0:QN], op=ALU.mult),
                              reads=[B_ps[bka], B_rt[qb]], writes=[B_t1])
                        bkb = misc()
                        group_mm(bkb, lambda k: w[:, k * 512 + 192:k * 512 + 256], rhsq, 4, QN, [B_whd[hb], B_qn[qb]], m=64)
                        fw.op("dve", lambda e: e.tensor_tensor(out=t2[:, 0:QN], in0=psum[bkb][0:64, 0:QN], in1=rt[qb][:, 512:512 + QN], op=ALU.mult),
                              reads=[B_ps[bkb], B_rt[qb]], writes=[B_t2])
                        fw.op("dve", lambda e: e.tensor_tensor(out=Qr[qb][0:64, 0:QN], in0=t1[:, 0:QN], in1=t2[:, 0:QN], op=ALU.add),
                              reads=[B_t1, B_t2], writes=[B_Qr[qb]])
                    else:
                        evac("dve", Qr[qb][0:64, 0:QN], psum[bka][0:64, 0:QN], [B_ps[bka]], [B_Qr[qb]])

                def attend(hd, item, qb, hook):
                    sq_, qt, QN = item
                    hb = hd % 2
                    tq = sq_["t0"] + qt * QN
                    nkt = (sq_["k1"] - sq_["k0"]) // 128
                    npair = nkt // 2
                    hookpi = max(npair // 2 - 1, 0)
                    v3 = lambda ap: ap.rearrange("p (a b) -> p a b", a=2)[:, :, 0:QN]

                    def s_pair(pi):
                        j = sc_[0] % 3
                        sc_[0] += 1
                        Bp = [B_ps[2 * j], B_ps[2 * j + 1]]

                        def mm(e):
                            for hf in range(2):
                                kc = sq_["k0"] + (2 * pi + hf) * 128
                                e.matmul(psum[2 * j + hf][:, 0:QN], KnT[hb][:, kc:kc + 128], Qn[qb][:, 0:QN], start=True, stop=False)
                                ins = e.matmul(psum[2 * j + hf][:, 0:QN], kpeT[:, kc:kc + 128], Qr[qb][:, 0:QN], start=False, stop=True)
                            return ins
                        fw.op("pe", mm, reads=[B_Kn[hb], B_Qn[qb], B_kpeT, B_Qr[qb]], writes=Bp)
                        pb = ptc[0] % 4
                        ptc[0] += 1
                        for hf in range(2):
                            fw.op("act", lambda e: e.activation(out=PT[pb][:, hf * 512:hf * 512 + QN], in_=psum[2 * j + hf][:, 0:QN], func=AF.Exp, scale=SCALE),
                                  reads=[Bp[hf]], writes=[B_PT[pb]])
                        if pi > hookpi:
                            return pb
                        eng, acc_, B_acc_ = ("dve", accA, B_accA) if pi % 2 == 0 else ("pool", accB, B_accB)
                        if pi < 2:
                            fw.op(eng, lambda e: e.tensor_copy(out=v3(acc_[:]), in_=v3(PT[pb][:])), reads=[B_PT[pb]], writes=[B_acc_])
                        else:
                            fw.op(eng, lambda e: e.tensor_tensor(out=v3(acc_[:]), in0=v3(acc_[:]), in1=v3(PT[pb][:]), op=ALU.add),
                                  reads=[B_PT[pb], B_acc_], writes=[B_acc_])
                        return pb

                    def pv_pair(pi, pb):
                        def mm(e):
                            for hf in range(2):
                                kt = 2 * pi + hf
                                vk = (sq_["k0"] // 128 + kt) * 128
                                ins = e.matmul(psum[6][:, 0:QN], Vh[hb][:, vk:vk + 128], PT[pb][:, hf * 512:hf * 512 + QN], start=(kt == 0), stop=(kt == nkt - 1))
                            if pi > hookpi:
                                for hf in range(2):
                                    ins = e.matmul(psum[7][:, 0:QN], ones_b[:], PT[pb][:, hf * 512:hf * 512 + QN],
                                                   start=(pi == hookpi + 1 and hf == 0), stop=False)
                            return ins
                        fw.op("pe", mm, reads=[B_V[hb], B_PT[pb], B_const], writes=[B_ps[6]] + ([B_ps[7]] if pi > hookpi else []))
                    pend = [s_pair(0)]
                    if npair > 1:
                        pend.append(s_pair(1))
                    for pi in range(npair):
                        if pi + 2 < npair:
                            pend.append(s_pair(pi + 2))
                        pv_pair(pi, pend[pi])
                        if pi == hookpi and hook is not None:
                            hook()
                    fw.op("dve", lambda e: e.tensor_tensor(out=yf[:, 0:QN], in0=psum[6][:, 0:QN], in1=smg[qb][:, 0:QN], op=ALU.mult),
                          reads=[B_ps[6], B_smg[qb]], writes=[B_yf])
                    fw.op("dve", lambda e: e.tensor_tensor(out=accA[:, 0:QN], in0=accA[:, 0:QN], in1=accA[:, 512:512 + QN], op=ALU.add),
                          reads=[B_accA], writes=[B_accA])
                    if hookpi >= 1:
                        fw.op("pool", lambda e: e.tensor_tensor(out=accB[:, 0:QN], in0=accB[:, 0:QN], in1=accB[:, 512:512 + QN], op=ALU.add),
                              reads=[B_accB], writes=[B_accB])
                        fw.op("dve", lambda e: e.tensor_tensor(out=accA[:, 0:QN], in0=accA[:, 0:QN], in1=accB[:, 0:QN], op=ALU.add),
                              reads=[B_accA, B_accB], writes=[B_accA])
                    fw.op("pe", lambda e: e.matmul(psum[7][:, 0:QN], ones_f[:], accA[:, 0:QN], start=(npair - 1 <= hookpi), stop=True),
                          reads=[B_const, B_accA], writes=[B_ps[7]])
                    fw.op("dve", lambda e: e.reciprocal(out=rl[:, 0:QN], in_=psum[7][:, 0:QN]), reads=[B_ps[7]], writes=[B_rl])
                    fw.op("dve", lambda e: e.tensor_tensor(out=yo[qb][:, 0:QN], in0=yf[:, 0:QN], in1=rl[:, 0:QN], op=ALU.mult),
                          reads=[B_yf, B_rl], writes=[B_yo[qb]])
                    fw.dma("pool", ymixT[D + hd * 128:D + (hd + 1) * 128, tq:tq + QN], yo[qb][:, 0:QN], reads=[B_yo[qb]], writes=[B_ymix])

                kv_proj(0)
                q_proj(0, items[0], 0)
                qcount = 0
                for hd in range(16):
                    if hd + 1 < 16:
                        fw.dma("sp", whd[(hd + 1) % 2][:], whv[hd + 1], reads=[wbuf("e_whd", i, (hd + 1) * 128)], writes=[B_whd[(hd + 1) % 2]])
                    for ii, item in enumerate(items):
                        qb = qcount % 2
                        qcount += 1

                        def hook(hd=hd, ii=ii, qb=qb):
                            bg_pump(1)
                            if ii == 4 and hd + 1 < 16:
                                kv_proj(hd + 1)
                            if ii + 1 < len(items):
                                q_proj(hd, items[ii + 1], 1 - qb)
                            elif hd + 1 < 16:
                                q_proj(hd + 1, items[0], 1 - qb)
                        attend(hd, item, qb, hook)
            fw.barrier()
        out_proj(l, e_wout_b[l // 2], "e_wout", l // 2)
        fw.barrier()

    def out_proj(l, wsrc, wname, wi):
        with ExitStack() as es:
            ym = [S(es, "ym%d" % j, [128, 32 * 512], BF16) for j in range(2)]; B_ym = [Buf(), Buf()]
            wo = [S(es, "wo%d" % j, [128, 32 * 256], BF16) for j in range(3)]; B_wo = [Buf(), Buf(), Buf()]
            xc = [S(es, "xo%d" % j, [128, 512]) for j in range(3)]; B_xc = [Buf() for _ in xc]
            wv = wsrc.rearrange("(q p) c -> q p c", p=128)
            yv = ymixT.rearrange("(c p) t -> p c t", p=128)
            pc = [0]; xcn = [0]; bkc = [0]
            for ti in range(T // 512):
                c0 = ti * 512
                cond = 0 if c0 < NS else 1
                yb = ti % 2
                for part in range(4):
                    fw.dma("sp", ym[yb][:].rearrange("p (c t) -> p c t", c=32)[:, part * 8:(part + 1) * 8, :], yv[:, part * 8:(part + 1) * 8, c0:c0 + 512],
                           reads=[B_ymix], writes=[B_ym[yb]])
                for pn in range(8):
                    sidx = ti * 8 + pn
                    while pc[0] <= sidx + 2 and pc[0] < (T // 512) * 8:
                        fw.dma("sp", wo[pc[0] % 3][:], wv[pc[0] % 8], reads=[wbuf(wname, wi, (pc[0] % 8) * 128)], writes=[B_wo[pc[0] % 3]])
                        pc[0] += 1
                        bg_pump(1)
                    b = sidx % 3
                    for jj in range(2):
                        j = pn * 2 + jj
                        r = xcn[0] % 3
                        xcn[0] += 1
                        fw.dma("sp", xc[r][:], xT[j * 128:(j + 1) * 128, c0:c0 + 512], reads=[B_xT], writes=[B_xc[r]])
                        bk = bkc[0] % 4
                        bkc[0] += 1
                        group_mm(bk, lambda k, b=b, jj=jj: wo[b][:, k * 256 + jj * 128:k * 256 + (jj + 1) * 128],
                                 lambda k, yb=yb: ym[yb][:, k * 512:(k + 1) * 512], 32, 512, [B_wo[b], B_ym[yb]])
                        fw.op("dve", lambda e, bk=bk, r=r, j=j, cond=cond: e.scalar_tensor_tensor(out=xc[r][:], in0=psum[bk][:, 0:512], scalar=mod_ap(GAt, l, j, cond),
                                                                                              in1=xc[r][:], op0=ALU.mult, op1=ALU.add),
                              reads=[B_ps[bk], B_xc[r], B_mod], writes=[B_xc[r]])
                        fw.dma("pool", xT[j * 128:(j + 1) * 128, c0:c0 + 512], xc[r][:], reads=[B_xc[r]], writes=[B_xT])

    def odd_layer(l):
        i = l // 2
        bg_pump(0, upto=l)
        oo = i * 64
        lng = lambda fc: osm[:, oo + fc:oo + fc + 1]
        lnb = lambda fc: osm[:, oo + 32 + fc:oo + 32 + fc + 1]
        with ExitStack() as es:
            xc = [S(es, "xc%d" % j, [128, 512]) for j in range(3)]; B_xc = [Buf() for _ in xc]
            sq = [S(es, "sq%d" % j, [128, 512]) for j in range(2)]; B_sq = [Buf(), Buf()]
            rstd = S(es, "rstd", [128, 512]); B_rstd = Buf()
            hT = [S(es, "hT%d" % j, [128, KC * 512], BF16) for j in range(1)]; B_h = [Buf()]
            wp = [S(es, "wp%d" % j, [128, KC * 512], BF16) for j in range(2)]; B_wp = [Buf(), Buf()]
            vb = S(es, "vb", [128, 4 * 4096], BF16); B_vb = Buf()
            wsf = S(es, "wsf", [128, 2048]); wsb = S(es, "wsb", [128, 2048], BF16); B_ws = Buf()
            wsc = S(es, "wsc", [128, 4 * 2048], BF16); B_wsc = Buf()
            nmr = S(es, "nmr", [128, 4 * 128], BF16); B_nmr = Buf()
            st_s = S(es, "st_s", [128, 4 * 8]); st_q = S(es, "st_q", [128, 4 * 8]); B_st = Buf()
            junk = S(es, "junk", [128, 512], BF16); B_junk = Buf()
            mu = S(es, "mu", [128, 16]); B_mu = Buf()
            Cc = S(es, "Cc", [128, 32 * 128]); B_Cc = Buf()
            yT = S(es, "yT", [128, 32 * 512], BF16); B_yT = Buf()
            sgt = S(es, "sgt", [128, 512]); ugt = [S(es, "ugt%d" % j, [128, 512]) for j in range(2)]; tmp = S(es, "tmp", [128, 512])
            B_sgt = Buf(); B_ugt = [Buf(), Buf()]; B_tmp = Buf()
            es_setup = ExitStack()
            bsb = S(es_setup, "bsb", [128, 2048]); rsb = S(es_setup, "rsb", [128, 2048])
            ctr = [0]; pc = [0]; bkc = [0]; woc = [0]

            fw.dma("sp", wsf[:], o_wsT[i], writes=[B_ws])
            fw.dma("sp", bsb[:], o_bs[i].partition_broadcast(128), writes=[B_ws])
            fw.op("dve", lambda e: e.tensor_copy(out=wsb[:], in_=wsf[:]), reads=[B_ws], writes=[B_ws])
            for q4 in range(4):
                fw.op("pe", lambda e, q4=q4: e.matmul(psum[q4][:, 0:512], ones_f[:], wsf[:, q4 * 512:(q4 + 1) * 512], start=True, stop=True),
                      reads=[B_ws, B_const], writes=[B_ps[q4]])
                fw.op("dve", lambda e, q4=q4: e.tensor_copy(out=rsb[:, q4 * 512:(q4 + 1) * 512], in_=psum[q4][:, 0:512]), reads=[B_ps[q4]], writes=[B_ws])
            for fc in range(32):
                g = fc // 2
                fw.op("dve", lambda e, fc=fc, g=g: e.scalar_tensor_tensor(out=Cc[:, fc * 128:(fc + 1) * 128], in0=rsb[:, g * 128:(g + 1) * 128], scalar=lnb(fc),
                                                                       in1=bsb[:, g * 128:(g + 1) * 128], op0=ALU.mult, op1=ALU.add),
                      reads=[B_ws, B_const], writes=[B_Cc])

            fw.barrier()
            es_setup.close()
            wo = [S(es, "wo%d" % j, [128, 32 * 128], BF16) for j in range(3)]; B_wo = [Buf(), Buf(), Buf()]
            wv = o_win_b[i].rearrange("(q p) c -> q p c", p=128)
            wov = o_wout_b[i].rearrange("(q p) c -> q p c", p=128)

            def load_panel(pn):
                b = pc[0] % 2
                pc[0] += 1
                fw.dma("sp", wp[b][:], wv[pn], reads=[wbuf("o_win", i, pn * 128)], writes=[B_wp[b]])
                bg_pump(1)
                return b

            def nextbank():
                bk = 1 + (bkc[0] % 5)
                bkc[0] += 1
                return bk

            for ti in range(T // 512):
                c0 = ti * 512
                cond = 0 if c0 < NS else 1
                n = 512
                hb = 0
                make_h(l, c0, n, cond, hT[0], B_h[0], xc, B_xc, sq, B_sq, rstd, B_rstd, 0, ctr)
                h = hT[hb]
                wb = load_panel(0)
                for pn in range(O_NP):
                    cur = wb
                    if pn + 1 < O_NP:
                        wb = load_panel(pn + 1)
                    w = wp[cur]
                    if pn < 8:
                        for tc in range(4):
                            bk = nextbank()
                            group_mm(bk, lambda k, tc=tc: h[:, k * 512 + tc * 128:k * 512 + (tc + 1) * 128], lambda k: w[:, k * 512:(k + 1) * 512],
                                     KC, 512, [B_wp[cur], B_h[hb]])
                            col = tc * 8 + pn
                            fw.op("act", lambda e, bk=bk, tc=tc, pn=pn, col=col: e.activation(out=vb[:, tc * 4096 + pn * 512:tc * 4096 + (pn + 1) * 512], in_=psum[bk][:, 0:512],
                                                                                         func=AF.Copy, accum_out=st_s[:, col:col + 1]),
                                  reads=[B_ps[bk]], writes=[B_vb, B_st])
                            fw.op("act", lambda e, bk=bk, col=col: e.activation(out=junk[:], in_=psum[bk][:, 0:512], func=AF.Square, accum_out=st_q[:, col:col + 1]),
                                  reads=[B_ps[bk]], writes=[B_junk, B_st])
                        if pn == 7:
                            fw.op("dve", lambda e: e.tensor_reduce(out=mu[:, 0:4], in_=st_s[:].rearrange("p (t c) -> p t c", c=8), axis=mybir.AxisListType.X, op=ALU.add),
                                  reads=[B_st], writes=[B_mu])
                            fw.op("dve", lambda e: e.tensor_reduce(out=mu[:, 4:8], in_=st_q[:].rearrange("p (t c) -> p t c", c=8), axis=mybir.AxisListType.X, op=ALU.add),
                                  reads=[B_st], writes=[B_mu])
                            fw.op("dve", lambda e: e.tensor_scalar(out=mu[:, 0:8], in0=mu[:, 0:8], scalar1=1.0 / 4096, scalar2=None, op0=ALU.mult),
                                  reads=[B_mu], writes=[B_mu])
                            fw.op("dve", lambda e: e.tensor_tensor(out=mu[:, 8:12], in0=mu[:, 0:4], in1=mu[:, 0:4], op=ALU.mult), reads=[B_mu], writes=[B_mu])
                            fw.op("dve", lambda e: e.tensor_tensor(out=mu[:, 4:8], in0=mu[:, 4:8], in1=mu[:, 8:12], op=ALU.subtract), reads=[B_mu], writes=[B_mu])
                            fw.op("act", lambda e: e.activation(out=mu[:, 4:8], in_=mu[:, 4:8], func=AF.Sqrt, bias=epsb[:, 0:1], scale=1.0), reads=[B_mu, B_const], writes=[B_mu])
                            fw.op("dve", lambda e: e.reciprocal(out=mu[:, 4:8], in_=mu[:, 4:8]), reads=[B_mu], writes=[B_mu])
                            fw.op("dve", lambda e: e.scalar_tensor_tensor(out=mu[:, 8:12], in0=mu[:, 0:4], scalar=-1.0, in1=mu[:, 4:8], op0=ALU.mult, op1=ALU.mult),
                                  reads=[B_mu], writes=[B_mu])
                            for tc in range(4):
                                fw.op("dve", lambda e, tc=tc: e.tensor_scalar(out=wsc[:, tc * 2048:(tc + 1) * 2048], in0=wsf[:], scalar1=mu[:, 4 + tc:5 + tc], scalar2=None, op0=ALU.mult),
                                      reads=[B_ws, B_mu], writes=[B_wsc])
                                fw.op("dve", lambda e, tc=tc: e.tensor_scalar(out=nmr[:, tc * 128:(tc + 1) * 128], in0=ones_f[:], scalar1=mu[:, 8 + tc:9 + tc], scalar2=None, op0=ALU.mult),
                                      reads=[B_const, B_mu], writes=[B_nmr])
                    else:
                        for half in range(2):
                            fc = (pn - 8) * 2 + half
                            g = fc // 2
                            bu, bg = nextbank(), nextbank()
                            rhs = lambda k: h[:, k * 512:(k + 1) * 512]
                            group_mm(bg, lambda k, half=half: w[:, k * 512 + half * 256 + 128:k * 512 + half * 256 + 256], rhs, KC, 512, [B_wp[cur], B_h[hb]])
                            group_mm(bu, lambda k, half=half: w[:, k * 512 + half * 256:k * 512 + half * 256 + 128], rhs, KC, 512, [B_wp[cur], B_h[hb]])
                            gb = 6 + fc % 2

                            def gate_mm(e, fc=fc, g=g, gb=gb):
                                for tc in range(4):
                                    e.matmul(psum[gb][:, tc * 128:(tc + 1) * 128], vb[:, tc * 4096 + fc * 128:tc * 4096 + (fc + 1) * 128],
                                             wsc[:, tc * 2048 + g * 128:tc * 2048 + (g + 1) * 128], start=True, stop=False)
                                    ins = e.matmul(psum[gb][:, tc * 128:(tc + 1) * 128], nmr[:, tc * 128:(tc + 1) * 128],
                                                   wsb[:, g * 128:(g + 1) * 128], start=False, stop=True)
                                return ins
                            fw.op("pe", gate_mm, reads=[B_vb, B_wsc, B_nmr, B_ws], writes=[B_ps[gb]])
                            fw.op("act", lambda e, bg=bg: e.activation(out=sgt[:], in_=psum[bg][:, 0:512], func=AF.Silu), reads=[B_ps[bg]], writes=[B_sgt])
                            ub = fc % 2
                            fw.op("dve", lambda e, bu=bu, ub=ub: e.tensor_tensor(out=ugt[ub][:], in0=psum[bu][:, 0:512], in1=sgt[:], op=ALU.mult),
                                  reads=[B_ps[bu], B_sgt], writes=[B_ugt[ub]])
                            cbc = Cc[:, fc * 128:(fc + 1) * 128].unsqueeze(1).to_broadcast([128, 4, 128])
                            fw.op("dve", lambda e, gb=gb, fc=fc, cbc=cbc: e.scalar_tensor_tensor(out=tmp[:].rearrange("p (a b) -> p a b", a=4), in0=psum[gb][:, 0:512].rearrange("p (a b) -> p a b", a=4),
                                                                                          scalar=lng(fc), in1=cbc, op0=ALU.mult, op1=ALU.add),
                                  reads=[B_ps[gb], B_Cc, B_const], writes=[B_tmp])
                            fw.op("dve", lambda e, fc=fc, ub=ub: e.tensor_tensor(out=yT[:, fc * 512:(fc + 1) * 512], in0=tmp[:], in1=ugt[ub][:], op=ALU.mult),
                                  reads=[B_tmp, B_ugt[ub]], writes=[B_yT])
                for j in range(KC):
                    b = woc[0] % 3
                    woc[0] += 1
                    fw.dma("sp", wo[b][:], wov[j], reads=[wbuf("o_wout", i, j * 128)], writes=[B_wo[b]])
                    r = ctr[0] % len(xc)
                    ctr[0] += 1
                    fw.dma("sp", xc[r][:], xT[j * 128:(j + 1) * 128, c0:c0 + 512], reads=[B_xT], writes=[B_xc[r]])
                    bk = nextbank()
                    group_mm(bk, lambda k, b=b: wo[b][:, k * 128:(k + 1) * 128], lambda k: yT[:, k * 512:(k + 1) * 512], 32, 512, [B_wo[b], B_yT])
                    fw.op("dve", lambda e, bk=bk, r=r, j=j, cond=cond: e.scalar_tensor_tensor(out=xc[r][:], in0=psum[bk][:, 0:512], scalar=mod_ap(GAt, l, j, cond),
                                                                                          in1=xc[r][:], op0=ALU.mult, op1=ALU.add),
                          reads=[B_ps[bk], B_xc[r], B_mod], writes=[B_xc[r]])
                    fw.dma("pool", xT[j * 128:(j + 1) * 128, c0:c0 + 512], xc[r][:], reads=[B_xc[r]], writes=[B_xT])
        fw.barrier()

    for l in range(depth if stop is None else 0):
        if l % 2 == 0:
            even_layer(l)
        else:
            odd_layer(l)

    with ExitStack() as es:
      if stop in (None, 'ada', 'final'):
          xc = [S(es, "xc%d" % j, [128, 512]) for j in range(3)]; B_xc = [Buf() for _ in xc]
          sq = [S(es, "sq%d" % j, [128, 512]) for j in range(2)]; B_sq = [Buf(), Buf()]
          rstd = S(es, "rstd", [128, 512]); B_rstd = Buf()
          ystg_t = [S(es, "ystg%d" % j, [128, D]) for j in range(4)]; B_ystg = [Buf() for _ in range(4)]
          ctr = [0]; bkc = [0]; sc_ = [0]
          for ti in range(T // 512):
              c0 = ti * 512
              for f in range(KC):
                  r = ctr[0] % 3
                  ctr[0] += 1
                  fw.dma("sp", xc[r][:], xT[f * 128:(f + 1) * 128, c0:c0 + 512], reads=[B_xT], writes=[B_xc[r]])
                  q = f % 2
                  fw.op("act", lambda e, r=r, q=q: e.activation(out=sq[q][:], in_=xc[r][:], func=AF.Square), reads=[B_xc[r]], writes=[B_sq[q]])
                  fw.op("pe", lambda e, q=q, f=f: e.matmul(psum[0][:, 0:512], ones_f[:], sq[q][:], start=(f == 0), stop=(f == KC - 1)),
                        reads=[B_sq[q], B_const], writes=[B_ps[0]])
              fw.op("act", lambda e: e.activation(out=rstd[:], in_=psum[0][:, 0:512], func=AF.Sqrt, bias=epsb[:, 0:1], scale=1.0 / D),
                    reads=[B_ps[0], B_const], writes=[B_rstd])
              fw.op("dve", lambda e: e.reciprocal(out=rstd[:], in_=rstd[:]), reads=[B_rstd], writes=[B_rstd])
              for f in range(KC):
                  r = ctr[0] % 3
                  ctr[0] += 1
                  fw.dma("sp", xc[r][:], xT[f * 128:(f + 1) * 128, c0:c0 + 512], reads=[B_xT], writes=[B_xc[r]])
                  fw.op("dve", lambda e, r=r, f=f: e.scalar_tensor_tensor(out=xc[r][:], in0=xc[r][:], scalar=ngs[:, 4 * KC + f:4 * KC + f + 1], in1=rstd[:], op0=ALU.mult, op1=ALU.mult),
                        reads=[B_xc[r], B_rstd, B_const], writes=[B_xc[r]])
                  for ts in range(4):
                      bk = 1 + bkc[0] % 7
                      bkc[0] += 1
                      fw.op("pe", lambda e, r=r, ts=ts, bk=bk: e.transpose(psum[bk][:, 0:128], xc[r][:, ts * 128:(ts + 1) * 128], ident[:]),
                            reads=[B_xc[r], B_const], writes=[B_ps[bk]])
                      sb = ystg_t[ts]
                      evac("act" if ts % 2 else "dve", sb[:, f * 128:(f + 1) * 128], psum[bk][:, 0:128], [B_ps[bk]], [B_ystg[ts]])
              for ts in range(4):
                  fw.dma("pool", y_tok[c0 + ts * 128:c0 + (ts + 1) * 128, :], ystg_t[ts][:], reads=[B_ystg[ts]], is_output=True)

    seen = fw.seen["pool"]
    for k, v in fw.out_toks:
        if seen.get(k, 0) < v:
            seen[k] = v
            nc.gpsimd.wait_ge(fw.sem[k], v)
    top.close()
    return nc


ROPE_PERM = np.concatenate([np.arange(16, 32), np.arange(0, 16), np.arange(48, 64), np.arange(32, 48)])
ROPE_SIGN = np.concatenate([-np.ones(16), np.ones(16), -np.ones(16), np.ones(16)]).astype(np.float32)


def _rope_tables():
    rows = NS // 64
    row = np.repeat(np.arange(rows, dtype=np.float32), 64)
    col = np.tile(np.arange(64, dtype=np.float32), rows)
    inv = (10000.0 ** (-np.arange(16, dtype=np.float32) / 16)).astype(np.float32)
    ar = row[:, None] * inv
    ac = col[:, None] * inv
    ang = np.concatenate([ar, ar, ac, ac], axis=-1)
    cos = np.cos(ang).astype(np.float32)
    sin = np.sin(ang).astype(np.float32)
    return np.ascontiguousarray(np.stack([cos.T, (sin * ROPE_SIGN[None, :]).T]))


def _pk(v, nchunk):
    return np.ascontiguousarray(v.reshape(nchunk, 128).T)


def _tile_w(w, cols_per_panel):
    K = w.shape[0]
    kc = K // 128
    W = len(cols_per_panel[0])
    out = np.zeros((len(cols_per_panel), 128, kc, W), dtype=np.float32)
    w3 = w.reshape(kc, 128, w.shape[1])
    for q, cols in enumerate(cols_per_panel):
        cols = np.asarray(cols)
        valid = cols >= 0
        out[q][:, :, valid] = w3[:, :, cols[valid]].transpose(1, 0, 2)
    return out.reshape(len(cols_per_panel) * 128, kc * W)


def _even_cols():
    CW = 2048
    panels = []
    for f in range(16):
        panels.append(np.concatenate([g * CW + f * 128 + np.arange(128) for g in range(4)]))
    panels.append(8192 + np.arange(512))
    panels.append(8704 + np.arange(512))
    for m in range(4):
        panels.append(9280 + m * 512 + np.arange(512))
    panels.append(np.concatenate([9216 + np.arange(64), 9216 + ROPE_PERM, -np.ones(384, dtype=np.int64)]))
    return panels


def _odd_cols():
    panels = [4096 + pn * 512 + np.arange(512) for pn in range(8)]
    for q in range(16):
        f0, f1 = 2 * q, 2 * q + 1
        panels.append(np.concatenate([f0 * 128 + np.arange(128), 8192 + f0 * 128 + np.arange(128),
                                      f1 * 128 + np.arange(128), 8192 + f1 * 128 + np.arange(128)]))
    return panels


def _head_w(wqb, wkvb):
    out = np.zeros((16, 128, 4, 512), dtype=np.float32)
    q3 = wqb.reshape(4, 128, 3072)
    k3 = wkvb.reshape(4, 128, 4096)
    for h in range(16):
        out[h][:, :, 0:192] = q3[:, :, h * 192:(h + 1) * 192].transpose(1, 0, 2)
        out[h][:, :, 192:256] = q3[:, :, h * 192 + 128 + ROPE_PERM].transpose(1, 0, 2)
        out[h][:, :, 256:512] = k3[:, :, h * 256:(h + 1) * 256].transpose(1, 0, 2)
    return out.reshape(16 * 128, 4 * 512)


_NC_CACHE = {}


def kernel(x_prompt, x_sample, cache_ckv, cache_kpe, c, c_ctx, norm_g, w_ada, b_ada,
           e_w_in, e_conv_w, e_q_norm_g, e_w_qb, e_kv_norm_g, e_w_kvb, e_w_out,
           o_w_in, o_ln_g, o_ln_b, o_w_s, o_b_s, o_w_out, final_g):
    f32 = lambda a: np.asarray(a, dtype=np.float32)
    x_prompt, x_sample, cache_ckv, cache_kpe, c, c_ctx = map(f32, (x_prompt, x_sample, cache_ckv, cache_kpe, c, c_ctx))
    norm_g, w_ada, b_ada, final_g = map(f32, (norm_g, w_ada, b_ada, final_g))
    ncores = 8
    if "nc" not in _NC_CACHE:
        _NC_CACHE["nc"] = build_program()
    nc = _NC_CACHE["nc"]

    shared = {}
    shared["ngT"] = np.ascontiguousarray(np.concatenate([_pk(norm_g[l], 16) for l in range(4)] + [_pk(final_g, 16)], axis=1))
    shared["w_ada"] = np.ascontiguousarray(w_ada)
    shared["b_adaT"] = np.ascontiguousarray(np.concatenate([_pk(b_ada[l], 48) for l in range(4)], axis=1))
    ecols, ocols = _even_cols(), _odd_cols()
    shared["e_win"] = np.stack([_tile_w(f32(e_w_in[i]), ecols) for i in range(2)])
    shared["e_whd"] = np.stack([_head_w(f32(e_w_qb[i]), f32(e_w_kvb[i])) for i in range(2)])
    wo_cols = [pn * 256 + np.arange(256) for pn in range(8)]
    shared["e_wout"] = np.stack([_tile_w(f32(e_w_out[i]), wo_cols) for i in range(2)])
    es_ = []
    for i in range(2):
        cw = f32(e_conv_w[i])
        cw_p = cw.reshape(3, 16, 128).transpose(2, 1, 0).reshape(128, 48)
        es_.append(np.concatenate([cw_p, _pk(f32(e_q_norm_g[i]), 4), _pk(f32(e_kv_norm_g[i]), 4)], axis=1))
    shared["e_small"] = np.ascontiguousarray(np.concatenate(es_, axis=1))
    shared["o_win"] = np.stack([_tile_w(f32(o_w_in[i]), ocols) for i in range(2)])
    shared["o_wout"] = np.stack([_tile_w(f32(o_w_out[i]), [j * 128 + np.arange(128) for j in range(16)]) for i in range(2)])
    shared["o_wsT"] = np.ascontiguousarray(np.stack([f32(o_w_s[i]).transpose(2, 0, 1).reshape(128, 2048) for i in range(2)]))
    shared["o_small"] = np.ascontiguousarray(np.concatenate(
        [np.concatenate([_pk(f32(o_ln_g[i]), 32), _pk(f32(o_ln_b[i]), 32)], axis=1) for i in range(2)], axis=1))
    shared["o_bs"] = np.ascontiguousarray(f32(o_b_s).reshape(2, 1, 2048))
    shared["ropeT"] = _rope_tables()
    shared["ident"] = np.eye(128, dtype=np.float32)

    in_maps = []
    for i in range(ncores):
        m = dict(shared)
        m["x_tok"] = np.ascontiguousarray(np.concatenate([x_sample[i], x_prompt[2 * i], x_prompt[2 * i + 1]], axis=0))
        m["cckv"] = np.ascontiguousarray(cache_ckv[i])
        m["ckpe"] = np.ascontiguousarray(cache_kpe[i])
        cond = np.stack([c[i], c_ctx], axis=1)
        m["condT"] = np.ascontiguousarray(cond.reshape(16, 128, 2).transpose(1, 0, 2).reshape(128, 32))
        in_maps.append(m)
    res = run_bass_kernel_spmd(nc, in_maps, core_ids=list(range(ncores)))
    y_prompt = np.zeros((16, 256, D), np.float32)
    y_sample = np.zeros((8, NS, D), np.float32)
    new_ckv = np.zeros((16, 2, 256, 512), np.float32)
    new_kpe = np.zeros((16, 2, 256, 64), np.float32)
    for i in range(ncores):
        r = res.results[i]
        y_sample[i] = r["y_tok"][:NS]
        y_prompt[2 * i] = r["y_tok"][NS:NS + 256]
        y_prompt[2 * i + 1] = r["y_tok"][NS + 256:]
        new_ckv[2 * i:2 * i + 2] = r["new_ckv"]
        new_kpe[2 * i:2 * i + 2] = r["new_kpe"]
    return (y_prompt, y_sample, new_ckv, new_kpe)
```
